# Optimizing a Trainium2 kernel written in Bass

```python
import math
import jax
import jax.numpy as jnp
from jax import lax
import numpy as np


D_MODEL = 2048
BATCH = 1
SEQ = 16384
DEPTH = 2

BRANCH_W = D_MODEL // 2
N_BRANCH = 4
NORM_EPS = 1e-6

HGRN_EXPAND = 128
HGRN_HEADS = BRANCH_W // HGRN_EXPAND
HGRN_CHUNK = 64

S5_GROUP = 16
S5_GROUPS = BRANCH_W // S5_GROUP
S5_STATE = 64
S5_DT_MIN = 1e-3
S5_DT_MAX = 1e-1
S5_MAX_RE = -1e-4

RWKV_HEAD = 64
RWKV_HEADS = BRANCH_W // RWKV_HEAD
RWKV_DECAY_LORA = 64
RWKV_A_LORA = 64
RWKV_SHIFT_W = 3 * BRANCH_W + RWKV_DECAY_LORA + RWKV_A_LORA
RWKV_DECAY_SCALE = 0.606531
RWKV_LN_EPS = 64e-5

M2_HEADDIM = 64
M2_HEADS = BRANCH_W // M2_HEADDIM
M2_GROUPS = 4
M2_STATE = 128
M2_CONV = 4
M2_CHUNK = 128
M2_CONV_CH = BRANCH_W + 2 * M2_GROUPS * M2_STATE
M2_NORM_EPS = 1e-5

IN_SPLITS = (BRANCH_W, BRANCH_W, BRANCH_W, BRANCH_W,
             BRANCH_W, BRANCH_W,
             RWKV_SHIFT_W, BRANCH_W,
             M2_CONV_CH, M2_HEADS, BRANCH_W,
             N_BRANCH * D_MODEL)
W_IN = sum(IN_SPLITS)

kernel_name = 'hybrid_hgrn2_s5_rwkv7_ssd_gated_block'

F32 = jnp.float32


def split_sizes(t, sizes):
    out, off = [], 0
    for s in sizes:
        out.append(t[..., off:off + s])
        off += s
    return out


def project_partitioned(h, w, sizes):
    out, off = [], 0
    for s in sizes:
        out.append(jnp.einsum('bld,de->ble', h, w[:, off:off + s].astype(F32)))
        off += s
    return out


def rmsnorm(x, g, eps=NORM_EPS):
    xf = x.astype(F32)
    return xf * lax.rsqrt(jnp.mean(xf * xf, axis=-1, keepdims=True) + eps) * g.astype(F32)


def group_rmsnorm(x, g, n_groups, eps):
    shp = x.shape
    xg = x.reshape(shp[:-1] + (n_groups, shp[-1] // n_groups))
    xg = xg * lax.rsqrt(jnp.mean(xg * xg, axis=-1, keepdims=True) + eps)
    return xg.reshape(shp) * g.astype(F32)


def token_shift(x):
    return jnp.pad(x, ((0, 0), (1, 0), (0, 0)))[:, :-1, :]


def causal_mask(n):
    return jnp.tril(jnp.ones((n, n), dtype=bool))


def gla_chunk_scan(q, k, v, log_f, chunk):
    bsz, seq, nh, dk = q.shape
    dv = v.shape[-1]
    n = seq // chunk

    def to_chunks(t):
        return t.reshape(bsz, n, chunk, nh, t.shape[-1]).transpose(1, 0, 3, 2, 4)

    mask = causal_mask(chunk)[:, :, None]

    def step(state, inp):
        qc, kc, vc, gc = inp
        cum = jnp.cumsum(gc, axis=2)
        o_inter = jnp.einsum('bhtk,bhkv->bhtv', qc * jnp.exp(cum), state)
        diff = cum[:, :, :, None, :] - cum[:, :, None, :, :]
        decay = jnp.exp(jnp.where(mask, diff, -jnp.inf))
        scores = jnp.einsum('bhtsk,bhsk->bhts', qc[:, :, :, None, :] * decay, kc)
        o_intra = jnp.einsum('bhts,bhsv->bhtv', scores, vc)
        last = cum[:, :, -1:, :]
        state = (jnp.exp(last[:, :, 0, :])[..., None] * state
                 + jnp.einsum('bhsk,bhsv->bhkv', kc * jnp.exp(last - cum), vc))
        return state, o_inter + o_intra

    s0 = jnp.zeros((bsz, nh, dk, dv), F32)
    _, o = lax.scan(step, s0, (to_chunks(q), to_chunks(k), to_chunks(v), to_chunks(log_f)))
    return o.transpose(1, 0, 3, 2, 4).reshape(bsz, seq, nh, dv)


def hgrn2_branch(q_raw, f_raw, i_raw, lb, norm_w):
    bsz, seq, w = q_raw.shape
    dv = w // HGRN_HEADS
    log_f = jnp.logaddexp(jnp.log(lb), jnp.log1p(-lb) + jax.nn.log_sigmoid(f_raw))
    k = (1.0 - lb) * jax.nn.sigmoid(-f_raw)
    q = jax.nn.silu(q_raw) * HGRN_EXPAND ** -0.5
    heads = lambda t, d: t.reshape(bsz, seq, HGRN_HEADS, d)
    o = gla_chunk_scan(heads(q, HGRN_EXPAND), heads(k, HGRN_EXPAND), heads(i_raw, dv),
                       heads(log_f, HGRN_EXPAND), HGRN_CHUNK)
    return group_rmsnorm(o.reshape(bsz, seq, w), norm_w, HGRN_HEADS, NORM_EPS)


def s5_branch(u, a_re, a_im, log_dt, b_re, b_im, c_re, c_im, d, w_glu, b_glu):
    bsz, seq, w = u.shape
    uf = u.reshape(bsz, seq, S5_GROUPS, S5_GROUP)
    lam_re = jnp.minimum(a_re.astype(F32), S5_MAX_RE)
    lam_im = a_im.astype(F32)
    dt = jnp.exp(log_dt.astype(F32))[:, None]
    dl_re, dl_im = lam_re * dt, lam_im * dt
    mag = jnp.exp(dl_re)
    num_re = mag * jnp.cos(dl_im) - 1.0
    num_im = mag * jnp.sin(dl_im)
    den = lam_re * lam_re + lam_im * lam_im
    coef_re = (num_re * lam_re + num_im * lam_im) / den
    coef_im = (num_im * lam_re - num_re * lam_im) / den
    b_re = b_re.astype(F32)
    b_im = b_im.astype(F32)
    bbar_re = coef_re[..., None] * b_re - coef_im[..., None] * b_im
    bbar_im = coef_re[..., None] * b_im + coef_im[..., None] * b_re
    bu_re = jnp.einsum('blgh,gph->blgp', uf, bbar_re)
    bu_im = jnp.einsum('blgh,gph->blgp', uf, bbar_im)
    counts = jnp.ones((1, seq, 1, 1), F32)

    def combine(e1, e2):
        n1, r1, i1 = e1
        n2, r2, i2 = e2
        m = jnp.exp(n2 * dl_re)
        ang = n2 * dl_im
        p_re, p_im = m * jnp.cos(ang), m * jnp.sin(ang)
        return (n1 + n2, p_re * r1 - p_im * i1 + r2, p_re * i1 + p_im * r1 + i2)

    _, s_re, s_im = lax.associative_scan(combine, (counts, bu_re, bu_im), axis=1)
    y = (jnp.einsum('blgp,ghp->blgh', s_re, c_re.astype(F32))
         - jnp.einsum('blgp,ghp->blgh', s_im, c_im.astype(F32)))
    y = (y + d.astype(F32).reshape(S5_GROUPS, S5_GROUP) * uf).reshape(bsz, seq, w)
    g = jax.nn.gelu(y)
    return g * jax.nn.sigmoid(g @ w_glu.astype(F32) + b_glu.astype(F32))


def rwkv7_branch(xs, mu, w0, w2, a0, a2, k_k, k_a, r_k, ln_w, ln_b):
    bsz, seq, _ = xs.shape
    nh, hd = RWKV_HEADS, RWKV_HEAD
    xs = xs + mu.astype(F32) * (token_shift(xs) - xs)
    r, k, v, wd, ad = split_sizes(xs, (BRANCH_W, BRANCH_W, BRANCH_W, RWKV_DECAY_LORA, RWKV_A_LORA))
    decay = jnp.exp(-RWKV_DECAY_SCALE * jax.nn.sigmoid(w0.astype(F32) + jnp.tanh(wd) @ w2.astype(F32)))
    iclr = jax.nn.sigmoid(a0.astype(F32) + ad @ a2.astype(F32))
    heads = lambda t: t.reshape(bsz, seq, nh, hd)
    r, k, v, decay, iclr = heads(r), heads(k), heads(v), heads(decay), heads(iclr)
    kk = k * k_k.astype(F32).reshape(nh, hd)
    kk = kk / jnp.maximum(jnp.sqrt(jnp.sum(kk * kk, axis=-1, keepdims=True)), 1e-12)
    k = k * (1.0 + (iclr - 1.0) * k_a.astype(F32).reshape(nh, hd))

    def step(state, inp):
        r_t, w_t, k_t, v_t, kk_t, a_t = inp
        sa = jnp.einsum('bhvk,bhk->bhv', state, -kk_t)
        state = (state * w_t[:, :, None, :]
                 + sa[..., None] * (kk_t * a_t)[:, :, None, :]
                 + v_t[..., None] * k_t[:, :, None, :])
        return state, jnp.einsum('bhvk,bhk->bhv', state, r_t)

    seq_major = lambda t: t.transpose(1, 0, 2, 3)
    s0 = jnp.zeros((bsz, nh, hd, hd), F32)
    _, y = lax.scan(step, s0, (seq_major(r), seq_major(decay), seq_major(k), seq_major(v),
                               seq_major(kk), seq_major(iclr)))
    y = y.transpose(1, 0, 2, 3)
    mean = jnp.mean(y, axis=-1, keepdims=True)
    var = jnp.mean(jnp.square(y - mean), axis=-1, keepdims=True)
    y = ((y - mean) * lax.rsqrt(var + RWKV_LN_EPS)).reshape(bsz, seq, BRANCH_W)
    y = y * ln_w.astype(F32) + ln_b.astype(F32)
    bonus = jnp.sum(r * k * r_k.astype(F32).reshape(nh, hd), axis=-1, keepdims=True) * v
    return y + bonus.reshape(bsz, seq, BRANCH_W)


def ssd_chunked(x, dt, a, bm, cm, chunk):
    bsz, seq, nh, hp = x.shape
    ng, ns = bm.shape[2], bm.shape[3]
    rep = nh // ng
    n = seq // chunk
    xd = (x * dt[..., None]).reshape(bsz, n, chunk, ng, rep, hp)
    a_cs = jnp.cumsum((dt * a).reshape(bsz, n, chunk, ng, rep), axis=2)
    bc = bm.reshape(bsz, n, chunk, ng, ns)
    cc = cm.reshape(bsz, n, chunk, ng, ns)
    seg = a_cs[:, :, :, None] - a_cs[:, :, None, :]
    decay = jnp.exp(jnp.where(causal_mask(chunk)[:, :, None, None], seg, -jnp.inf))
    cb = jnp.einsum('bclgn,bcsgn->bclsg', cc, bc)
    y_diag = jnp.einsum('bclsgr,bcsgrp->bclgrp', cb[..., None] * decay, xd)
    to_end = jnp.exp(a_cs[:, :, -1:] - a_cs)
    states = jnp.einsum('bcsgn,bcsgrp->bcgrpn', bc, xd * to_end[..., None])
    chunk_decay = jnp.exp(a_cs[:, :, -1])

    def step(state, inp):
        st, dec = inp
        return dec[..., None, None] * state + st, state

    s0 = jnp.zeros((bsz, ng, rep, hp, ns), F32)
    _, prev = lax.scan(step, s0, (jnp.moveaxis(states, 1, 0), jnp.moveaxis(chunk_decay, 1, 0)))
    prev = jnp.moveaxis(prev, 0, 1)
    y_off = jnp.einsum('bclgn,bcgrpn->bclgrp', cc, prev) * jnp.exp(a_cs)[..., None]
    return (y_diag + y_off).reshape(bsz, seq, nh, hp)


def mamba2_branch(xbc, dt_raw, z, conv_w, conv_b, dt_bias, a_log, d, norm_w):
    bsz, seq, _ = xbc.shape
    xbc = lax.conv_general_dilated(xbc.astype(F32), conv_w.astype(F32)[:, None, :],
                                   window_strides=(1,), padding=((M2_CONV - 1, 0),),
                                   dimension_numbers=('NWC', 'WIO', 'NWC'),
                                   feature_group_count=M2_CONV_CH)
    xbc = jax.nn.silu(xbc + conv_b.astype(F32))
    xh, bm, cm = split_sizes(xbc, (BRANCH_W, M2_GROUPS * M2_STATE, M2_GROUPS * M2_STATE))
    dt = jax.nn.softplus(dt_raw + dt_bias.astype(F32))
    a = -jnp.exp(a_log.astype(F32))
    xh = xh.reshape(bsz, seq, M2_HEADS, M2_HEADDIM)
    y = ssd_chunked(xh, dt, a, bm.reshape(bsz, seq, M2_GROUPS, M2_STATE),
                    cm.reshape(bsz, seq, M2_GROUPS, M2_STATE), M2_CHUNK)
    y = y + d.astype(F32)[:, None] * xh
    y = y.reshape(bsz, seq, BRANCH_W) * jax.nn.silu(z)
    return group_rmsnorm(y, norm_w, M2_GROUPS, M2_NORM_EPS)


def setup_inputs(seed: int = 0) -> dict:
    key = jax.random.key(seed)
    ks = jax.random.split(key, 40)
    W = BRANCH_W

    def nrm(i, shape, scale):
        return scale * jax.random.normal(ks[i], shape, F32)

    def uni(i, shape, lo, hi):
        return jax.random.uniform(ks[i], shape, F32, lo, hi)

    s5_im_init = jnp.pi * jnp.arange(S5_STATE, dtype=F32)
    dt_m2 = jnp.exp(uni(31, (DEPTH, M2_HEADS), math.log(1e-3), math.log(1e-1)))
    return {
        'x': nrm(0, (BATCH, SEQ, D_MODEL), 1.0),
        'norm_pre': 1.0 + nrm(1, (DEPTH, D_MODEL), 0.02),
        'norm_post': 1.0 + nrm(2, (DEPTH, D_MODEL), 0.02),
        'w_in': nrm(3, (DEPTH, D_MODEL, W_IN), D_MODEL ** -0.5),
        'gate_bias': nrm(4, (DEPTH, N_BRANCH, D_MODEL), 0.01),
        'w_up': nrm(5, (DEPTH, N_BRANCH, W, D_MODEL), W ** -0.5),
        'w_out': nrm(6, (DEPTH, D_MODEL, D_MODEL), D_MODEL ** -0.5),
        'hgrn_lb_logits': nrm(7, (DEPTH, W), 0.1),
        'hgrn_norm': 1.0 + nrm(8, (DEPTH, W), 0.02),
        's5_a_re': -0.5 + nrm(9, (DEPTH, S5_GROUPS, S5_STATE), 0.01),
        's5_a_im': s5_im_init + nrm(10, (DEPTH, S5_GROUPS, S5_STATE), 0.01),
        's5_log_dt': uni(11, (DEPTH, S5_GROUPS), math.log(S5_DT_MIN), math.log(S5_DT_MAX)),
        's5_b_re': nrm(12, (DEPTH, S5_GROUPS, S5_STATE, S5_GROUP), (2 * S5_GROUP) ** -0.5),
        's5_b_im': nrm(13, (DEPTH, S5_GROUPS, S5_STATE, S5_GROUP), (2 * S5_GROUP) ** -0.5),
        's5_c_re': nrm(14, (DEPTH, S5_GROUPS, S5_GROUP, S5_STATE), (2 * S5_STATE) ** -0.5),
        's5_c_im': nrm(15, (DEPTH, S5_GROUPS, S5_GROUP, S5_STATE), (2 * S5_STATE) ** -0.5),
        's5_d': nrm(16, (DEPTH, W), 1.0),
        's5_w_glu': nrm(17, (DEPTH, W, W), W ** -0.5),
        's5_b_glu': nrm(18, (DEPTH, W), 0.01),
        'rwkv_mu': uni(19, (DEPTH, RWKV_SHIFT_W), 0.0, 1.0),
        'rwkv_w0': uni(20, (DEPTH, W), -3.0, 3.0),
        'rwkv_w2': nrm(21, (DEPTH, RWKV_DECAY_LORA, W), 0.1),
        'rwkv_a0': nrm(22, (DEPTH, W), 0.5),
        'rwkv_a2': nrm(23, (DEPTH, RWKV_A_LORA, W), 0.1),
        'rwkv_k_k': 0.85 + nrm(24, (DEPTH, W), 0.02),
        'rwkv_k_a': 1.0 + nrm(25, (DEPTH, W), 0.02),
        'rwkv_r_k': nrm(26, (DEPTH, W), 0.1),
        'rwkv_ln_w': 1.0 + nrm(27, (DEPTH, W), 0.02),
        'rwkv_ln_b': nrm(28, (DEPTH, W), 0.01),
        'm2_conv_w': nrm(29, (DEPTH, M2_CONV, M2_CONV_CH), M2_CONV ** -0.5),
        'm2_conv_b': nrm(30, (DEPTH, M2_CONV_CH), 0.01),
        'm2_dt_bias': dt_m2 + jnp.log(-jnp.expm1(-dt_m2)),
        'm2_a_log': jnp.log(uni(32, (DEPTH, M2_HEADS), 1.0, 16.0)),
        'm2_d': 1.0 + nrm(33, (DEPTH, M2_HEADS), 0.1),
        'm2_norm': 1.0 + nrm(34, (DEPTH, W), 0.02),
    }


def reference(x, norm_pre, norm_post, w_in, gate_bias, w_up, w_out, hgrn_lb_logits, hgrn_norm,
              s5_a_re, s5_a_im, s5_log_dt, s5_b_re, s5_b_im, s5_c_re, s5_c_im, s5_d, s5_w_glu,
              s5_b_glu, rwkv_mu, rwkv_w0, rwkv_w2, rwkv_a0, rwkv_a2, rwkv_k_k, rwkv_k_a, rwkv_r_k,
              rwkv_ln_w, rwkv_ln_b, m2_conv_w, m2_conv_b, m2_dt_bias, m2_a_log, m2_d, m2_norm):
    lb_all = jnp.cumsum(jax.nn.softmax(hgrn_lb_logits.astype(F32), axis=0), axis=0)
    lb_all = lb_all - lb_all[0:1]
    res = x
    for l in range(DEPTH):
        h = rmsnorm(res, norm_pre[l])
        (q_a, f_a, i_a, z_a, u_b, z_b, rkvwa_c, z_c, xbc_d, dt_d, z_d,
         gate_pre) = project_partitioned(h, w_in[l], IN_SPLITS)
        o_a = hgrn2_branch(q_a, f_a, i_a, lb_all[l], hgrn_norm[l]) * jax.nn.silu(z_a)
        o_b = s5_branch(u_b, s5_a_re[l], s5_a_im[l], s5_log_dt[l], s5_b_re[l], s5_b_im[l],
                        s5_c_re[l], s5_c_im[l], s5_d[l], s5_w_glu[l], s5_b_glu[l]) * jax.nn.silu(z_b)
        o_c = rwkv7_branch(rkvwa_c, rwkv_mu[l], rwkv_w0[l], rwkv_w2[l], rwkv_a0[l], rwkv_a2[l],
                           rwkv_k_k[l], rwkv_k_a[l], rwkv_r_k[l], rwkv_ln_w[l], rwkv_ln_b[l]) * jax.nn.silu(z_c)
        o_d = mamba2_branch(xbc_d, dt_d, z_d, m2_conv_w[l], m2_conv_b[l], m2_dt_bias[l],
                            m2_a_log[l], m2_d[l], m2_norm[l])
        outs = (o_a, o_b, o_c, o_d)
        merged = None
        for b in range(N_BRANCH):
            gate = jax.nn.sigmoid(gate_pre[..., b * D_MODEL:(b + 1) * D_MODEL] + gate_bias[l, b].astype(F32))
            contrib = gate * jnp.einsum('blw,wd->bld', outs[b], w_up[l, b].astype(F32))
            merged = contrib if merged is None else merged + contrib
        y = jnp.einsum('bld,de->ble', merged, w_out[l].astype(F32))
        res = res + rmsnorm(y, norm_post[l])
    return res.astype(x.dtype)
```

```python
import numpy as np
from contextlib import ExitStack
import concourse.bass as bass
import concourse.mybir as mybir
from concourse.bass_utils import run_bass_kernel_spmd

F32 = mybir.dt.float32
BF16 = mybir.dt.bfloat16
I32 = mybir.dt.int32
ALU = mybir.AluOpType
AF = mybir.ActivationFunctionType

D_MODEL = 2048
W = 1024
TB = 512
NW1 = 14 * 128 + 2
PI = 3.14159265358979


class Buf:
    def __init__(self, t, name):
        self.t = t
        self.name = name
        self.w = None
        self.r = {}

    def __getitem__(self, k):
        return self.t[k]


class Prog:
    ENGS = ["pe", "act", "dve", "pool", "sp"]

    def __init__(self, nc, es):
        self.nc = nc
        self.es = es
        self.q = {e: [] for e in self.ENGS}
        self.semobj = {}
        for e in self.ENGS:
            self.semobj[e] = es.enter_context(nc.semaphore("s_" + e))
        self.cnt = {e: 0 for e in self.ENGS}
        self.seen = {e: {} for e in self.ENGS}
        self.dcnt = {}
        self.nbuf = 0
        self.ninst = 0
        self.sbytes = 0

    def sb(self, shape, dt=F32, name=None):
        self.nbuf += 1
        name = name or f"sb{self.nbuf}"
        t = self.es.enter_context(self.nc.sbuf_tensor(name, list(shape), dt))
        n = 1
        for s in shape[1:]:
            n *= s
        self.sbytes += n * (4 if dt in (F32, I32) else 2)
        return Buf(t, name)

    def ps(self, shape, dt=F32, name=None):
        self.nbuf += 1
        name = name or f"ps{self.nbuf}"
        t = self.es.enter_context(self.nc.psum_tensor(name, list(shape), dt))
        return Buf(t, name)

    def dram(self, name, shape, dt, kind):
        t = self.nc.dram_tensor(name, list(shape), dt, kind=kind)
        return Buf(t.ap(), name)

    def _waits(self, e, reads, writes):
        need = {}

        def add(ev, raw):
            if ev is None:
                return
            k, v = ev
            if k == e:
                if e == "pe":
                    return
            if need.get(k, 0) < v:
                need[k] = v

        reads = [getattr(b, "base", b) for b in reads]
        writes = [getattr(b, "base", b) for b in writes]
        for b in reads:
            add(b.w, True)
        for b in writes:
            add(b.w, False)
            for k, v in b.r.items():
                add((k, v), False)
        for k, v in need.items():
            if self.seen[e].get(k, 0) >= v:
                continue
            self.seen[e][k] = v
            s = self.semobj[k]
            self.q[e].append(lambda eng, s=s, v=v: eng.wait_ge(s, v))
            self.ninst += 1

    def _record(self, ev, reads, writes):
        k, v = ev
        reads = [getattr(b, "base", b) for b in reads]
        writes = [getattr(b, "base", b) for b in writes]
        for b in reads:
            if b.r.get(k, 0) < v:
                b.r[k] = v
        for b in writes:
            b.w = ev
            b.r = {}

    def op(self, e, fn, reads=(), writes=()):
        self._waits(e, reads, writes)
        self.cnt[e] += 1
        s = self.semobj[e]
        self.q[e].append(lambda eng, fn=fn, s=s: fn(eng).then_inc(s, 1))
        self.ninst += 1
        self._record((e, self.cnt[e]), reads, writes)

    def group(self, e, fns, reads=(), writes=()):
        self._waits(e, reads, writes)
        self.cnt[e] += 1
        s = self.semobj[e]
        for f in fns[:-1]:
            self.q[e].append(lambda eng, f=f: f(eng))
        self.q[e].append(lambda eng, f=fns[-1], s=s: f(eng).then_inc(s, 1))
        self.ninst += len(fns)
        self._record((e, self.cnt[e]), reads, writes)

    def dma(self, e, out_ap, in_ap, reads=(), writes=(), key=None):
        if key is None:
            key = "d_" + (writes[0].name if writes else reads[0].name)
        if key not in self.semobj:
            self.semobj[key] = self.es.enter_context(self.nc.semaphore(key))
            self.dcnt[key] = 0
        self._waits(e, reads, writes)
        self.dcnt[key] += 16
        s = self.semobj[key]
        self.q[e].append(lambda eng, o=out_ap, i=in_ap, s=s: eng.dma_start(out=o, in_=i).then_inc(s, 16))
        self.ninst += 1
        self._record((key, self.dcnt[key]), reads, writes)

    def finish(self, out_bufs):
        for b in out_bufs:
            self._waits("sp", [b], [b])
        for k, v in self.dcnt.items():
            if self.seen["sp"].get(k, 0) < v:
                self.seen["sp"][k] = v
                self.q["sp"].append(lambda eng, s=self.semobj[k], v=v: eng.wait_ge(s, v))
        nc = self.nc
        with nc.Block() as block:

            @block.tensor
            def _(eng):
                for f in self.q["pe"]:
                    f(eng)

            @block.scalar
            def _(eng):
                for f in self.q["act"]:
                    f(eng)

            @block.vector
            def _(eng):
                for f in self.q["dve"]:
                    f(eng)

            @block.gpsimd
            def _(eng):
                for f in self.q["pool"]:
                    f(eng)

            @block.sync
            def _(eng):
                for f in self.q["sp"]:
                    f(eng)

    def tt(self, e, out, a, b, op, R, Wr):
        self.op(e, lambda g: g.tensor_tensor(out, a, b, op), R, Wr)

    def ts(self, e, out, a, s1, s2, op0, op1, R, Wr):
        if s2 is None:
            self.op(e, lambda g: g.tensor_scalar(out, a, s1, None, op0), R, Wr)
        else:
            self.op(e, lambda g: g.tensor_scalar(out, a, s1, s2, op0, op1), R, Wr)

    def stt(self, e, out, a, sc, b, op0, op1, R, Wr):
        self.op(e, lambda g: g.scalar_tensor_tensor(out, a, sc, b, op0, op1), R, Wr)

    def act(self, out, a, func, R, Wr, bias=None, scale=None, accum=None):
        kw = {}
        if bias is not None:
            kw["bias"] = bias
        if scale is not None:
            kw["scale"] = scale
        if accum is not None:
            kw["accum_out"] = accum
        self.op("act", lambda g: g.activation(out=out, in_=a, func=func, **kw), R, Wr)

    def cp(self, e, out, a, R, Wr):
        if e == "act":
            self.op(e, lambda g: g.copy(out, a), R, Wr)
        else:
            self.op(e, lambda g: g.tensor_copy(out, a), R, Wr)

    def fence(self, e):
        v = self.cnt[e]
        if v > 0 and self.seen[e].get(e, 0) < v:
            self.seen[e][e] = v
            self.q[e].append(lambda eng, s=self.semobj[e], v=v: eng.wait_ge(s, v))

    def mm(self, out, lhsT, rhs, R, Wr, start=True, stop=True, fence=False):
        if fence:
            self.fence("pe")
        self.op("pe", lambda g: g.matmul(out, lhsT, rhs, start=start, stop=stop), R, Wr)


C_ID, C_TRI, C_NEG, C_BONES, C_MH, C_MR, C_M01, C_IOTA, C_ONES, C_I4, NCST = 0, 128, 256, 384, 512, 1024, 2048, 2560, 3072, 3200, 3456


def make_consts():
    c = np.zeros((128, NCST), np.float32)
    i = np.arange(128)
    c[:, C_ID:C_ID + 128] = np.eye(128)
    c[:, C_TRI:C_TRI + 128] = (i[:, None] <= i[None, :])
    c[:, C_NEG:C_NEG + 128] = np.where(i[None, :] >= i[:, None], 0.0, -1e5)
    c[:, C_BONES:C_BONES + 128] = ((i[:, None] // 64) == (i[None, :] // 64))
    j = np.arange(64)
    incl = (j[:, None] <= j[None, :]).astype(np.float32)
    strict = (j[:, None] < j[None, :]).astype(np.float32)
    c[0:64, C_MH:C_MH + 512] = np.tile(incl, (1, 8))
    c[0:64, C_MR:C_MR + 512] = np.tile(strict, (1, 8))
    c[0:64, C_MR + 512:C_MR + 1024] = np.tile(strict.T, (1, 8))
    t = np.arange(512)
    c[:, C_M01:C_M01 + 512] = (t % 64 != 0)[None, :]
    c[:, C_IOTA:C_IOTA + 512] = t[None, :]
    c[:, C_ONES:C_ONES + 128] = 1.0
    c[0:64, C_I4:C_I4 + 256] = np.tile(np.eye(64, dtype=np.float32), (1, 4))
    return c


(V_LB0, V_LB1, V_HNORM, V_S5D, V_MU, V_W0, V_A0, V_KK, V_KA, V_RK, V_LNW, V_LNB,
 V_CWX, V_CBX, V_CWB, V_CBB, V_CWC, V_CBC, V_M2D, V_S5) = (0, 1, 2, 3, 4, 8, 9, 10, 11, 12, 13, 14,
                                                            15, 19, 20, 24, 25, 29, 30, 31)
V_SEL = 43
NV = 31 + 12 + 1


def prep_phase1(inp, l, c):
    cs = slice(c * 128, (c + 1) * 128)
    w_in = inp["w_in"][l]
    off = {}
    o = 0
    names = ["q_a", "f_a", "i_a", "z_a", "u_b", "z_b", "rkvwa", "z_c", "xbc", "dt", "z_d", "gate"]
    sizes = [W, W, W, W, W, W, 3 * W + 128, W, W + 1024, 16, W, 4 * D_MODEL]
    for n, s in zip(names, sizes):
        off[n] = o
        o += s
    g = c // 2
    cols = []
    for base in (off["q_a"], off["f_a"], off["i_a"], off["z_a"], off["u_b"],
                 off["rkvwa"], off["rkvwa"] + W, off["rkvwa"] + 2 * W):
        cols.append(np.arange(base + c * 128, base + (c + 1) * 128))
    cols.append(np.arange(off["rkvwa"] + 3 * W, off["rkvwa"] + 3 * W + 128))
    cols.append(np.arange(off["z_c"] + c * 128, off["z_c"] + (c + 1) * 128))
    cols.append(np.arange(off["xbc"] + c * 128, off["xbc"] + (c + 1) * 128))
    cols.append(np.arange(off["xbc"] + W + g * 128, off["xbc"] + W + (g + 1) * 128))
    cols.append(np.arange(off["xbc"] + W + 512 + g * 128, off["xbc"] + W + 512 + (g + 1) * 128))
    cols.append(np.arange(off["z_d"] + c * 128, off["z_d"] + (c + 1) * 128))
    cols.append(np.arange(off["dt"] + 2 * c, off["dt"] + 2 * c + 2))
    cols = np.concatenate(cols)
    w1 = np.ascontiguousarray(w_in[:, cols])
    vec = np.zeros((128, NV), np.float32)
    vec[:, V_LB0] = inp["hgrn_lb_logits"][0][cs]
    vec[:, V_LB1] = inp["hgrn_lb_logits"][1][cs]
    vec[:, V_SEL] = float(l)
    vec[:, V_HNORM] = inp["hgrn_norm"][l][cs]
    vec[:, V_S5D] = inp["s5_d"][l][cs]
    mu = inp["rwkv_mu"][l]
    for k in range(3):
        vec[:, V_MU + k] = mu[k * W + c * 128:k * W + (c + 1) * 128]
    vec[:, V_MU + 3] = mu[3 * W:3 * W + 128]
    for k, n in ((V_W0, "rwkv_w0"), (V_A0, "rwkv_a0"), (V_KK, "rwkv_k_k"), (V_KA, "rwkv_k_a"),
                 (V_RK, "rwkv_r_k"), (V_LNW, "rwkv_ln_w"), (V_LNB, "rwkv_ln_b")):
        vec[:, k] = inp[n][l][cs]
    cw, cb = inp["m2_conv_w"][l], inp["m2_conv_b"][l]
    for k, sl in ((V_CWX, cs), (V_CWB, slice(W + g * 128, W + (g + 1) * 128)),
                  (V_CWC, slice(W + 512 + g * 128, W + 512 + (g + 1) * 128))):
        for j in range(4):
            vec[:, k + j] = cw[j, sl]
        vec[:, k + 4] = cb[sl]
    vec[:, V_M2D] = np.repeat(inp["m2_d"][l][2 * c:2 * c + 2], 64)
    for i in range(4):
        gs = slice(8 * c + 2 * i, 8 * c + 2 * i + 2)
        vec[:, V_S5 + 3 * i] = inp["s5_a_re"][l][gs].reshape(128)
        vec[:, V_S5 + 3 * i + 1] = inp["s5_a_im"][l][gs].reshape(128)
        vec[:, V_S5 + 3 * i + 2] = np.repeat(inp["s5_log_dt"][l][gs], 64)
    rows = np.concatenate([inp["m2_dt_bias"][l][2 * c:2 * c + 2], inp["m2_a_log"][l][2 * c:2 * c + 2]])[None, :]
    lora = np.concatenate([inp["rwkv_w2"][l][:, cs], inp["rwkv_a2"][l][:, cs]], 0)
    s5b = np.zeros((4, 128, 256), np.float32)
    s5c = np.zeros((4, 128, 256), np.float32)
    for i in range(4):
        for gg in range(2):
            G = 8 * c + 2 * i + gg
            gl = 2 * i + gg
            s5b[i, gg * 64:(gg + 1) * 64, gl * 16:(gl + 1) * 16] = inp["s5_b_re"][l][G]
            s5b[i, gg * 64:(gg + 1) * 64, 128 + gl * 16:128 + (gl + 1) * 16] = inp["s5_b_im"][l][G]
            s5c[i, gg * 64:(gg + 1) * 64, gl * 16:(gl + 1) * 16] = inp["s5_c_re"][l][G].T
            s5c[i, gg * 64:(gg + 1) * 64, 128 + gl * 16:128 + (gl + 1) * 16] = inp["s5_c_im"][l][G].T
    gpre = np.ascontiguousarray(inp["norm_pre"][l].reshape(16, 128).T)
    return dict(w1=w1, vecs=vec, rows=np.ascontiguousarray(rows.astype(np.float32)), lora=np.ascontiguousarray(lora),
                s5b=s5b, s5c=s5c, gpre=gpre)


NF, NH = 15, 9


def build_phase1(T, lyr=0, flags=(1, 1, 1, 1)):
    NB = T // TB
    nc = bass.Bass("TRN2", target_bir_lowering=False)
    with ExitStack() as es:
        P = Prog(nc, es)
        x_d = P.dram("xres", [T, D_MODEL], F32, "ExternalInput")
        w_d = P.dram("w1", [D_MODEL, NW1], F32, "ExternalInput")
        gp_d = P.dram("gpre", [128, 16], F32, "ExternalInput")
        vec_d = P.dram("vecs", [128, NV], F32, "ExternalInput")
        row_d = P.dram("rows", [1, 4], F32, "ExternalInput")
        lora_d = P.dram("lora", [128, 128], F32, "ExternalInput")
        s5b_d = P.dram("s5b", [4, 128, 256], F32, "ExternalInput")
        s5c_d = P.dram("s5c", [4, 128, 256], F32, "ExternalInput")
        cst_d = P.dram("consts", [128, NCST], F32, "ExternalInput")
        out_d = P.dram("outs", [4, 128, T], F32, "ExternalOutput")

        cst = P.sb([128, NCST], F32, "cst")
        vec = P.sb([128, NV], F32, "vec")
        gp = P.sb([128, 16], F32, "gp")
        rowb = P.sb([128, 4], F32, "rowb")
        P.dma("sp", cst[:], cst_d[:], writes=[cst])
        P.dma("sp", vec[:], vec_d[:], writes=[vec])
        P.dma("sp", gp[:], gp_d[:], writes=[gp])
        P.dma("sp", rowb[:], row_d.t.partition_broadcast(128), writes=[rowb])
        idb = P.sb([128, 128], BF16, "idb")
        P.cp("dve", idb[:], cst[:, C_ID:C_ID + 128], [cst], [idb])

        def V(k):
            return vec[:, k:k + 1]

        xt = [P.sb([128, D_MODEL], F32, f"xt{i}") for i in range(2)]
        wbf = P.sb([128, 16, NW1], BF16, "wbf")
        w_v = w_d.t.rearrange("(c p) n -> p c n", p=128)
        for kc in range(16):
            st = xt[kc % 2]
            P.dma("sp", st[:, 0:NW1], w_v[:, kc, :], writes=[st])
            P.ts("pool" if kc % 2 else "dve", wbf[:, kc, :], st[:, 0:NW1], gp[:, kc:kc + 1], None, ALU.mult, None, [st, gp], [wbf])

        pa = [P.ps([128, 512], F32, f"pa{i}") for i in range(2)]
        pt = [P.ps([128, 512], F32, f"pt{i}") for i in range(2)]
        pm = [P.ps([128, 512], F32, f"pm{i}") for i in range(2)]
        pqb = [P.ps([128, 512], F32, f"pqb{i}") for i in range(2)]
        pq = [Buf(pqb[i // 4].t[:, (i % 4) * 128:(i % 4 + 1) * 128], f"pq{i}") for i in range(8)]
        for i in range(8):
            pq[i].base = pqb[i // 4]

        xn = P.sb([128, D_MODEL], BF16, "xn")
        hT = P.sb([128, 16, TB], BF16, "hT")
        ss = P.sb([128, 1], F32, "ss")
        rstd = P.sb([128, 1], F32, "rstd")
        eps6 = P.sb([128, 1], F32, "eps6")
        P.op("pool", lambda g: g.memset(eps6[:], 1e-6), [], [eps6])
        dtp = P.sb([128, 4, 2], F32, "dtp")
        Fs = [P.sb([128, TB + 4], F32, f"F{i}") for i in range(NF)]
        Hs = [P.sb([128, 2 * TB], BF16, f"H{i}") for i in range(NH)]

        ctx = dict(P=P, cst=cst, vec=vec, V=V, pa=pa, pt=pt, pm=pm, pq=pq, pqb=pqb, F=Fs, H=Hs, idb=idb, rowb=rowb,
                   dtp=dtp, S={}, lyr=lyr, lora_d=lora_d, s5b_d=s5b_d, s5c_d=s5c_d, eps6=eps6, out_d=out_d,
                   hT=hT, wbf=wbf)
        setups = (hgrn_setup, s5_setup, rwkv_setup, ssd_setup)
        blocks = (hgrn_block, s5_block, rwkv_block, ssd_block)
        for b in range(4):
            if flags[b]:
                setups[b](ctx)

        def proj(ci, ncol=128):
            p = pa[ci % 2]
            fns = [lambda g, kc=kc, p=p: g.matmul(p[0:ncol, :], wbf[:, kc, ci * 128:ci * 128 + ncol], hT[:, kc, :],
                                                  start=(kc == 0), stop=(kc == 15)) for kc in range(16)]
            P.group("pe", fns, [wbf, hT], [p])
            return p

        def evac(ci, dst, func=None, eng="act", off=0):
            p = proj(ci)
            if func is not None:
                P.act(dst[:, off:off + TB], p[:], func, [p], [dst])
            else:
                P.cp(eng, dst[:, off:off + TB], p[:], [p], [dst])

        ctx["evac"] = evac

        for blk in range(NB):
            t0 = blk * TB
            ctx["t0"] = t0
            for j in range(4):
                x = xt[j % 2]
                P.dma("sp", x[:], x_d[t0 + j * 128:t0 + (j + 1) * 128, :], writes=[x])
                P.op("dve", lambda g: g.memset(ss[:], 0.0), [], [ss])
                P.act(xn[:], x[:], AF.Square, [x, ss], [xn, ss], accum=ss[:])
                P.act(rstd[:], ss[:], AF.Sqrt, [ss, eps6], [rstd], bias=eps6[:], scale=1.0 / D_MODEL)
                P.op("dve", lambda g: g.reciprocal(rstd[:], rstd[:]), [rstd], [rstd])
                P.ts("dve", xn[:], x[:], rstd[:], None, ALU.mult, None, [x, rstd], [xn])
                for k4 in range(4):
                    p = pt[k4 % 2]
                    fns = [lambda g, kc=kc, p=p, k4=k4: g.matmul(p[:, (kc - 4 * k4) * 128:(kc - 4 * k4 + 1) * 128],
                                                                 xn[:, kc * 128:(kc + 1) * 128], idb[:], start=True, stop=True)
                           for kc in range(4 * k4, 4 * k4 + 4)]
                    P.group("pe", fns, [xn, idb], [p])
                    P.cp("act" if k4 % 2 else "dve", hT[:, 4 * k4:4 * k4 + 4, j * 128:(j + 1) * 128],
                         p[:].rearrange("p (a b) -> p a b", b=128), [p], [hT])
            for b in range(4):
                if flags[b]:
                    blocks[b](ctx, blk)
        P.finish([out_d])
        print("phase1: ninst", P.ninst, "sbuf bytes/partition", P.sbytes)
    return nc


def store_out(c, b, buf):
    P, t0 = c["P"], c["t0"]
    P.dma("pool", c["out_d"].t[b, :, t0:t0 + TB], buf[:, 0:TB], reads=[buf], writes=[c["out_d"]], key="d_out")


def hgrn_setup(c):
    P, V = c["P"], c["V"]
    S = c["S"]
    lb = P.sb([128, 1], F32, "h_lb")
    oml = P.sb([128, 1], F32, "h_oml")
    noml = P.sb([128, 1], F32, "h_noml")
    P.tt("dve", lb[:], V(V_LB1), V(V_LB0), ALU.subtract, [c["vec"]], [lb])
    P.act(lb[:], lb[:], AF.Sigmoid, [lb], [lb])
    P.tt("dve", lb[:], lb[:], V(V_SEL), ALU.mult, [lb, c["vec"]], [lb])
    P.ts("dve", oml[:], lb[:], -1.0, 1.0, ALU.mult, ALU.add, [lb], [oml])
    P.ts("dve", noml[:], oml[:], -1.0, None, ALU.mult, None, [oml], [noml])
    S["h_lb"], S["h_oml"], S["h_noml"] = lb, oml, noml
    S["h_S32"] = P.sb([128, 128], F32, "h_S32")
    S["h_Sbf"] = P.sb([128, 128], BF16, "h_Sbf")
    P.op("pool", lambda g: g.memset(S["h_S32"][:], 0.0), [], [S["h_S32"]])
    P.op("pool", lambda g: g.memset(S["h_Sbf"][:], 0.0), [], [S["h_Sbf"]])


def hgrn_block(c, blk):
    P, V, cst, S, pm, pq, F, H = c["P"], c["V"], c["cst"], c["S"], c["pm"], c["pq"], c["F"], c["H"]
    vec, idb, evac = c["vec"], c["idb"], c["evac"]
    lb, oml, noml = S["h_lb"], S["h_oml"], S["h_noml"]
    qs, sig, za, f, b, eb, enb, kt32 = F[0], F[1], F[2], F[3], F[4], F[5], F[6], F[7]
    iv, Qt, Kt, Kh, Vt, KT, PT = H[0], H[1], H[2], H[3], H[4], H[5], H[6]
    o32, sq = F[3], F[4]
    S32, Sbf = S["h_S32"], S["h_Sbf"]
    evac(0, qs, AF.Silu)
    evac(1, sig, AF.Sigmoid)
    evac(2, iv, None, "dve")
    evac(3, za, AF.Silu)
    m01 = cst[:, C_M01:C_M01 + TB]
    N = slice(0, TB)
    P.ts("dve", f[:, N], sig[:, N], oml[:], lb[:], ALU.mult, ALU.add, [sig, oml, lb], [f])
    P.act(f[:, N], f[:, N], AF.Ln, [f], [f])
    P.op("dve", lambda g: g.tensor_tensor_scan(b[:, N], m01, f[:, N], 0.0, ALU.mult, ALU.add), [cst, f], [b])
    P.act(eb[:, N], b[:, N], AF.Exp, [b], [eb])
    P.act(enb[:, N], b[:, N], AF.Exp, [b], [enb], scale=-1.0)
    P.ts("dve", kt32[:, N], sig[:, N], noml[:], oml[:], ALU.mult, ALU.add, [sig, noml, oml], [kt32])
    P.tt("dve", kt32[:, N], kt32[:, N], enb[:, N], ALU.mult, [kt32, enb], [kt32])
    P.stt("dve", Qt[:, N], qs[:, N], float(128 ** -0.5), eb[:, N], ALU.mult, ALU.mult, [qs, eb], [Qt])
    P.cp("pool", Kt[:, N], kt32[:, N], [kt32], [Kt])
    k3 = kt32.t[:, 0:TB].rearrange("p (c s) -> p c s", s=64)
    e3 = eb.t[:, 0:TB].rearrange("p (c s) -> p c s", s=64)
    kh3 = Kh.t[:, 0:TB].rearrange("p (c s) -> p c s", s=64)
    P.tt("dve", kh3, k3, e3[:, :, 63:64].broadcast_to([128, 8, 64]), ALU.mult, [kt32, eb], [Kh])
    for (src, dst, pp) in ((iv, Vt, pm[0]), (Kh, KT, pm[1])):
        for half in range(2):
            fns = [lambda g, cc=cc, src=src, pp=pp, half=half: g.matmul(pp[0:64, (cc - 4 * half) * 128:(cc - 4 * half + 1) * 128],
                                                                       src[:, cc * 64:(cc + 1) * 64], idb[:], start=True, stop=True)
                   for cc in range(4 * half, 4 * half + 4)]
            P.group("pe", fns, [src, idb], [pp])
            P.cp("act" if half else "dve", dst[0:64, half * 512:(half + 1) * 512], pp[0:64, :], [pp], [dst])
    sc = pm[0]
    fns = [lambda g, cc=cc: g.matmul(sc[0:64, cc * 64:(cc + 1) * 64], Kt[:, cc * 64:(cc + 1) * 64], Qt[:, cc * 64:(cc + 1) * 64],
                                     start=True, stop=True) for cc in range(8)]
    P.group("pe", fns, [Kt, Qt], [sc])
    P.tt("dve", PT[0:64, N], sc[0:64, :], cst[0:64, C_MH:C_MH + 512], ALU.mult, [sc, cst], [PT])
    oT = pm[1]
    for cc in range(8):
        cs_ = slice(cc * 64, (cc + 1) * 64)
        vs_ = slice(cc * 128, (cc + 1) * 128)
        P.mm(oT[:, cs_], Vt[0:64, vs_], PT[0:64, cs_], [Vt, PT], [oT], start=True, stop=False)
        P.mm(oT[:, cs_], Sbf[:], Qt[:, cs_], [Sbf, Qt], [oT], start=False, stop=True)
        dS = pq[cc % 2]
        P.mm(dS[:], KT[0:64, vs_], Vt[0:64, vs_], [KT, Vt], [dS])
        P.stt("dve", S32[:], S32[:], eb[:, cc * 64 + 63:cc * 64 + 64], dS[:], ALU.mult, ALU.add, [S32, eb, dS], [S32])
        P.cp("act", Sbf[:], S32[:], [S32], [Sbf])
    P.cp("act", o32[:, N], oT[:], [oT], [o32])
    P.tt("pool", sq[:, N], o32[:, N], o32[:, N], ALU.mult, [o32], [sq])
    ssb = pm[0]
    P.mm(ssb[:], cst[:, C_ONES:C_ONES + 128], sq[:, N], [cst, sq], [ssb])
    P.act(sq[:, N], ssb[:], AF.Sqrt, [ssb, c["eps6"]], [sq], bias=c["eps6"][:], scale=1.0 / 128)
    P.op("dve", lambda g: g.reciprocal(sq[:, N], sq[:, N]), [sq], [sq])
    P.stt("dve", o32[:, N], o32[:, N], V(V_HNORM), sq[:, N], ALU.mult, ALU.mult, [o32, vec, sq], [o32])
    P.tt("dve", o32[:, N], o32[:, N], za[:, N], ALU.mult, [o32, za], [o32])
    store_out(c, 0, o32)


def sincos_frac(P, out, tin, ti, R, n):
    P.cp("dve", ti[:, 0:n], tin[:, 0:n], [tin], [ti])
    P.cp("dve", out[:, 0:n], ti[:, 0:n], [ti], [out])
    P.tt("dve", tin[:, 0:n], tin[:, 0:n], out[:, 0:n], ALU.subtract, [tin, out], [tin])
    P.act(out[:, 0:n], tin[:, 0:n], AF.Sin, [tin], [out], scale=6.28318)


def s5_setup(c):
    P, V, cst, S, F, pq = c["P"], c["V"], c["cst"], c["S"], c["F"], c["pq"]
    vec = c["vec"]
    v3 = vec.t[:, V_S5:V_S5 + 12].rearrange("p (i k) -> p i k", k=3)
    a_re, a_im, ldt = v3[:, :, 0], v3[:, :, 1], v3[:, :, 2]
    sm = {n: P.sb([128, 4], F32, "s5_" + n) for n in
          ("lre", "dt", "rho", "th2", "cos", "sin", "c512", "s512", "t1", "t2", "t3", "den", "cre", "cim", "ncim", "stre", "stim", "lr", "li")}
    ti = Buf(F[14].t.bitcast(I32), "s5_ti")
    ti.base = F[14]
    lre, dt, rho, th2 = sm["lre"], sm["dt"], sm["rho"], sm["th2"]
    P.ts("dve", lre[:], a_re, -1e-4, None, ALU.min, None, [vec], [lre])
    P.act(dt[:], ldt, AF.Exp, [vec], [dt])
    P.tt("dve", rho[:], lre[:], dt[:], ALU.mult, [lre, dt], [rho])
    P.act(rho[:], rho[:], AF.Exp, [rho], [rho])
    P.tt("dve", th2[:], a_im, dt[:], ALU.mult, [vec, dt], [th2])
    P.ts("dve", th2[:], th2[:], float(1.0 / (2 * PI)), None, ALU.mult, None, [th2], [th2])
    t1, t2, t3 = sm["t1"], sm["t2"], sm["t3"]
    P.cp("dve", t1[:], th2[:], [th2], [t1])
    sincos_frac(P, sm["sin"], t1, ti, None, 4)
    P.ts("dve", t1[:], th2[:], 0.25, None, ALU.add, None, [th2], [t1])
    sincos_frac(P, sm["cos"], t1, ti, None, 4)
    P.ts("dve", t1[:], th2[:], 512.0, None, ALU.mult, None, [th2], [t1])
    sincos_frac(P, sm["s512"], t1, ti, None, 4)
    P.ts("dve", t1[:], th2[:], 512.0, 0.25, ALU.mult, ALU.add, [th2], [t1])
    sincos_frac(P, sm["c512"], t1, ti, None, 4)
    P.tt("dve", t1[:], rho[:], sm["cos"][:], ALU.mult, [rho, sm["cos"]], [t1])
    P.ts("dve", t1[:], t1[:], -1.0, None, ALU.add, None, [t1], [t1])
    P.tt("dve", t2[:], rho[:], sm["sin"][:], ALU.mult, [rho, sm["sin"]], [t2])
    den = sm["den"]
    P.tt("dve", den[:], lre[:], lre[:], ALU.mult, [lre], [den])
    P.tt("dve", t3[:], a_im, a_im, ALU.mult, [vec], [t3])
    P.tt("dve", den[:], den[:], t3[:], ALU.add, [den, t3], [den])
    P.op("dve", lambda g: g.reciprocal(den[:], den[:]), [den], [den])
    cre, cim, ncim = sm["cre"], sm["cim"], sm["ncim"]
    P.tt("dve", cre[:], t1[:], lre[:], ALU.mult, [t1, lre], [cre])
    P.tt("dve", t3[:], t2[:], a_im, ALU.mult, [t2, vec], [t3])
    P.tt("dve", cre[:], cre[:], t3[:], ALU.add, [cre, t3], [cre])
    P.tt("dve", cre[:], cre[:], den[:], ALU.mult, [cre, den], [cre])
    P.tt("dve", cim[:], t2[:], lre[:], ALU.mult, [t2, lre], [cim])
    P.tt("dve", t3[:], t1[:], a_im, ALU.mult, [t1, vec], [t3])
    P.tt("dve", cim[:], cim[:], t3[:], ALU.subtract, [cim, t3], [cim])
    P.tt("dve", cim[:], cim[:], den[:], ALU.mult, [cim, den], [cim])
    P.ts("dve", ncim[:], cim[:], -1.0, None, ALU.mult, None, [cim], [ncim])
    P.op("pool", lambda g: g.memset(sm["stre"][:], 0.0), [], [sm["stre"]])
    P.op("pool", lambda g: g.memset(sm["stim"][:], 0.0), [], [sm["stim"]])
    S["s5"] = sm
    BT = P.sb([128, 4, 2, 128], BF16, "s5_BT")
    CB = P.sb([128, 4, 2, 128], BF16, "s5_CB")
    S["s5_BT"], S["s5_CB"] = BT, CB
    bst, bb = F[0], F[1]
    ident = cst[:, C_ID:C_ID + 128]
    for i in range(4):
        P.dma("sp", bst[:, 0:256], c["s5b_d"].t[i], writes=[bst])
        P.ts("dve", bb[:, 0:128], bst[:, 0:128], cre[:, i:i + 1], None, ALU.mult, None, [bst, cre], [bb])
        P.stt("dve", bb[:, 0:128], bst[:, 128:256], ncim[:, i:i + 1], bb[:, 0:128], ALU.mult, ALU.add, [bst, ncim, bb], [bb])
        P.ts("dve", bb[:, 128:256], bst[:, 128:256], cre[:, i:i + 1], None, ALU.mult, None, [bst, cre], [bb])
        P.stt("dve", bb[:, 128:256], bst[:, 0:128], cim[:, i:i + 1], bb[:, 128:256], ALU.mult, ALU.add, [bst, cim, bb], [bb])
        for r in range(2):
            P.mm(pq[r][:], bb[:, r * 128:(r + 1) * 128], ident, [bb, cst], [pq[r]])
            P.cp("dve", BT[:, i, r, :], pq[r][:], [pq[r]], [BT])
        P.dma("sp", bst[:, 0:256], c["s5c_d"].t[i], writes=[bst])
        P.cp("dve", CB[:, i, 0, :], bst[:, 0:128], [bst], [CB])
        P.ts("dve", CB[:, i, 1, :], bst[:, 128:256], -1.0, None, ALU.mult, None, [bst], [CB])
    tab = P.sb([128, 4, 2, TB], F32, "s5_tab")
    rht = P.sb([128, 4, TB], F32, "s5_rhot")
    S["s5_tab"], S["s5_rht"] = tab, rht
    iota = cst[:, C_IOTA:C_IOTA + TB]
    tt_, to_ = F[2], F[3]
    for i in range(4):
        P.ts("dve", tt_[:, 0:TB], iota, th2[:, i:i + 1], 0.25, ALU.mult, ALU.add, [cst, th2], [tt_])
        sincos_frac(P, to_, tt_, ti, None, TB)
        P.cp("pool", tab[:, i, 0, :], to_[:, 0:TB], [to_], [tab])
        P.ts("dve", tt_[:, 0:TB], iota, th2[:, i:i + 1], None, ALU.mult, None, [cst, th2], [tt_])
        sincos_frac(P, to_, tt_, ti, None, TB)
        P.cp("pool", tab[:, i, 1, :], to_[:, 0:TB], [to_], [tab])
        P.ts("dve", rht[:, i, :], cst[:, C_ONES:C_ONES + 1].broadcast_to([128, TB]), rho[:, i:i + 1], None, ALU.mult, None, [cst, rho], [rht])


def s5_block(c, blk):
    import os
    if os.environ.get("S5SKIP") == "blk":
        return
    P, V, cst, S, pm, pq, F, H = c["P"], c["V"], c["cst"], c["S"], c["pm"], c["pq"], c["F"], c["H"]
    vec, evac = c["vec"], c["evac"]
    sm, BT, CB, tab, rht = S["s5"], S["s5_BT"], S["s5_CB"], S["s5_tab"], S["s5_rht"]
    N = slice(0, TB)
    u, ub = F[0], H[0]
    evac(4, u, None, "dve")
    P.cp("act", ub[:, N], u[:, N], [u], [ub])
    ybank = c["pqb"][0]
    YB = pq[0:4]
    for i in range(4):
        cosT, sinT = tab[:, i, 0, :], tab[:, i, 1, :]
        bre, bim = pm[0], pm[1]
        P.mm(bre[:], BT[:, i, 0, :], ub[:, N], [BT, ub], [bre])
        P.mm(bim[:], BT[:, i, 1, :], ub[:, N], [BT, ub], [bim])
        cr, ci, t3, t4, xr, xi = F[1], F[2], F[3], F[4], F[5], F[6]
        P.tt("dve", cr[:, N], bre[:], cosT, ALU.mult, [bre, tab], [cr])
        P.tt("dve", t3[:, N], bim[:], sinT, ALU.mult, [bim, tab], [t3])
        P.tt("pool", cr[:, N], cr[:, N], t3[:, N], ALU.add, [cr, t3], [cr])
        P.tt("dve", ci[:, N], bim[:], cosT, ALU.mult, [bim, tab], [ci])
        P.tt("dve", t4[:, N], bre[:], sinT, ALU.mult, [bre, tab], [t4])
        P.tt("pool", ci[:, N], ci[:, N], t4[:, N], ALU.subtract, [ci, t4], [ci])
        P.op("dve", lambda g, i=i: g.tensor_tensor_scan(xr[:, N], rht[:, i, :], cr[:, N], sm["stre"][:, i:i + 1], ALU.mult, ALU.add),
             [rht, cr, sm["stre"]], [xr])
        P.op("dve", lambda g, i=i: g.tensor_tensor_scan(xi[:, N], rht[:, i, :], ci[:, N], sm["stim"][:, i:i + 1], ALU.mult, ALU.add),
             [rht, ci, sm["stim"]], [xi])
        t1 = sm["t1"]
        P.ts("dve", t1[:, 0:1], xi[:, TB - 1:TB], sm["s512"][:, i:i + 1], None, ALU.mult, None, [xi, sm["s512"]], [t1])
        P.stt("dve", sm["stre"][:, i:i + 1], xr[:, TB - 1:TB], sm["c512"][:, i:i + 1], t1[:, 0:1], ALU.mult, ALU.subtract,
              [xr, sm["c512"], t1], [sm["stre"]])
        P.ts("dve", t1[:, 1:2], xi[:, TB - 1:TB], sm["c512"][:, i:i + 1], None, ALU.mult, None, [xi, sm["c512"]], [t1])
        P.stt("dve", sm["stim"][:, i:i + 1], xr[:, TB - 1:TB], sm["s512"][:, i:i + 1], t1[:, 1:2], ALU.mult, ALU.add,
              [xr, sm["s512"], t1], [sm["stim"]])
        P.tt("pool", cr[:, N], xr[:, N], cosT, ALU.mult, [xr, tab], [cr])
        P.tt("pool", t3[:, N], xi[:, N], sinT, ALU.mult, [xi, tab], [t3])
        P.tt("dve", H[1][:, N], cr[:, N], t3[:, N], ALU.subtract, [cr, t3], [H[1]])
        P.tt("pool", ci[:, N], xr[:, N], sinT, ALU.mult, [xr, tab], [ci])
        P.tt("pool", t4[:, N], xi[:, N], cosT, ALU.mult, [xi, tab], [t4])
        P.tt("dve", H[2][:, N], ci[:, N], t4[:, N], ALU.add, [ci, t4], [H[2]])
        P.mm(ybank[:], CB[:, i, 0, :], H[1][:, N], [CB, H[1]], YB, start=(i == 0), stop=False)
        P.mm(ybank[:], CB[:, i, 1, :], H[2][:, N], [CB, H[2]], YB, start=False, stop=(i == 3))
    y = F[7]
    P.stt("dve", y[:, N], u[:, N], V(V_S5D), ybank[:], ALU.mult, ALU.add, [u, vec] + YB, [y])
    P.act(y[:, N], y[:, N], AF.Gelu_apprx_tanh, [y], [y])
    store_out(c, 1, y)


def ssd_setup(c):
    P, S = c["P"], c["S"]
    rowb = c["rowb"]
    S["d_Aneg"] = P.sb([128, 2], F32, "d_Aneg")
    P.act(S["d_Aneg"][:], rowb[:, 2:4], AF.Exp, [rowb], [S["d_Aneg"]])
    P.ts("dve", S["d_Aneg"][:], S["d_Aneg"][:], -1.0, None, ALU.mult, None, [S["d_Aneg"]], [S["d_Aneg"]])
    S["d_S32"] = P.sb([128, 128], F32, "d_S32")
    S["d_Sbf"] = P.sb([128, 128], BF16, "d_Sbf")
    P.op("pool", lambda g: g.memset(S["d_S32"][:], 0.0), [], [S["d_S32"]])
    P.op("pool", lambda g: g.memset(S["d_Sbf"][:], 0.0), [], [S["d_Sbf"]])
    S["d_halo"] = P.sb([128, 3, 3], F32, "d_halo")
    P.op("pool", lambda g: g.memset(S["d_halo"][:], 0.0), [], [S["d_halo"]])
    for n in ("dt", "at", "lcol", "llast", "elast", "wte"):
        S["d_" + n] = P.sb([128, 4, 2], F32, "d_" + n)


def ssd_block(c, blk):
    P, V, cst, S, pm, pq, F, H = c["P"], c["V"], c["cst"], c["S"], c["pm"], c["pq"], c["F"], c["H"]
    vec, evac, idb, rowb, dtp = c["vec"], c["evac"], c["idb"], c["rowb"], c["dtp"]
    hT, wbf = c["hT"], c["wbf"]
    N = slice(0, TB)
    xin, Bin, Cin, zd, xc, Ba, Cc, Dm, yo = F[0], F[1], F[2], F[3], F[4], F[5], F[6], F[7], F[8]
    BT, CT, X2, M2 = H[0], H[1], H[2], H[3]
    halo = S["d_halo"]
    evac(10, xin, None, "dve", 3)
    evac(11, Bin, None, "act", 3)
    evac(12, Cin, None, "dve", 3)
    evac(13, zd, AF.Silu)
    p = c["pa"][0]
    fns = []
    for j in range(4):
        for kc in range(16):
            fns.append(lambda g, j=j, kc=kc, p=p: g.matmul(p[:, j * 2:j * 2 + 2], hT[:, kc, j * 128:(j + 1) * 128],
                                                           wbf[:, kc, 14 * 128:14 * 128 + 2], start=(kc == 0), stop=(kc == 15)))
    P.group("pe", fns, [wbf, hT], [p])
    P.cp("dve", dtp[:].rearrange("p a b -> p (a b)"), p[:, 0:8], [p], [dtp])
    for k, (src, acc, vw) in enumerate(((xin, xc, V_CWX), (Bin, Ba, V_CWB), (Cin, Cc, V_CWC))):
        P.cp("pool", src[:, 0:3], halo[:, k, :], [halo], [src])
        P.cp("pool", halo[:, k, :], src[:, TB:TB + 3], [src], [halo])
        P.ts("dve", acc[:, N], src[:, 0:TB], V(vw), None, ALU.mult, None, [src, vec], [acc])
        for j in range(1, 4):
            P.stt("dve", acc[:, N], src[:, j:j + TB], V(vw + j), acc[:, N], ALU.mult, ALU.add, [src, vec, acc], [acc])
    P.act(xc[:, N], xc[:, N], AF.Silu, [xc, vec], [xc], bias=V(V_CWX + 4))
    P.act(BT[:, N], Ba[:, N], AF.Silu, [Ba, vec], [BT], bias=V(V_CWB + 4))
    P.act(Cc[:, N], Cc[:, N], AF.Silu, [Cc, vec], [Cc], bias=V(V_CWC + 4))
    P.cp("pool", CT[:, N], Cc[:, N], [Cc], [CT])
    dt, at, lcol, llast, elast, wte = (S["d_" + n] for n in ("dt", "at", "lcol", "llast", "elast", "wte"))
    P.tt("dve", dt[:], dtp[:], rowb[:, 0:2].unsqueeze(1).broadcast_to([128, 4, 2]), ALU.add, [dtp, rowb], [dt])
    P.act(dt[:], dt[:], AF.Exp, [dt], [dt])
    P.act(dt[:], dt[:], AF.Ln, [dt], [dt], bias=1.0)
    P.tt("dve", at[:], dt[:], S["d_Aneg"][:].unsqueeze(1).broadcast_to([128, 4, 2]), ALU.mult, [dt, S["d_Aneg"]], [at])
    tri = cst[:, C_TRI:C_TRI + 128]
    ident = cst[:, C_ID:C_ID + 128]
    neg = cst[:, C_NEG:C_NEG + 128]
    S32, Sbf = S["d_S32"], S["d_Sbf"]
    for j in range(4):
        ts_ = slice(j * 128, (j + 1) * 128)
        P.mm(pq[0][:, 0:2], tri, at[:, j, :], [cst, at], [pq[0]])
        for h in range(2):
            P.mm(pq[1 + h][:], at[:, j, h:h + 1].broadcast_to([128, 128]), tri, [at, cst], [pq[1 + h]])
        P.mm(pq[3][:], BT[:, ts_], CT[:, ts_], [BT, CT], [pq[3]])
        P.mm(pq[4][:], xc[:, ts_], ident, [xc, cst], [pq[4]])
        P.mm(pq[5][:], BT[:, ts_], idb[:], [BT, idb], [pq[5]])
        P.cp("act", X2[:, 256:384], pq[5][:], [pq[5]], [X2])
        P.cp("dve", lcol[:, j, :], pq[0][:, 0:2], [pq[0]], [lcol])
        for h in range(2):
            P.cp("dve", llast[:, j, h:h + 1], pq[1 + h][:, 127:128], [pq[1 + h]], [llast])
        P.act(elast[:, j, :], llast[:, j, :], AF.Exp, [llast], [elast])
        for h in range(2):
            P.act(wte[:, j, h:h + 1], lcol[:, j, h:h + 1], AF.Exp, [lcol, llast], [wte], bias=llast[:, j, h:h + 1], scale=-1.0)
        P.tt("dve", wte[:, j, :], wte[:, j, :], dt[:, j, :], ALU.mult, [wte, dt], [wte])
        for h in range(2):
            hs = slice(h * 64, (h + 1) * 64)
            P.ts("dve", X2[:, h * 64:(h + 1) * 64], pq[4][:, hs], dt[:, j, h:h + 1], None, ALU.mult, None, [pq[4], dt], [X2])
            P.ts("dve", X2[:, 128 + h * 64:128 + (h + 1) * 64], pq[4][:, hs], wte[:, j, h:h + 1], None, ALU.mult, None, [pq[4], wte], [X2])
            d1 = Dm[:, h * 128:(h + 1) * 128]
            P.stt("dve", d1, pq[1 + h][:], lcol[:, j, h:h + 1], neg, ALU.subtract, ALU.add, [pq[1 + h], lcol, cst], [Dm])
            P.act(d1, d1, AF.Exp, [Dm], [Dm])
            P.tt("dve", M2[:, h * 128:(h + 1) * 128], pq[3][:], d1, ALU.mult, [pq[3], Dm], [M2])
            e1 = Dm[:, 256 + h * 128:256 + (h + 1) * 128]
            P.act(e1, pq[1 + h][:], AF.Exp, [pq[1 + h]], [Dm])
            P.tt("pool", M2[:, 256 + h * 128:256 + (h + 1) * 128], Cc[:, ts_], e1, ALU.mult, [Cc, Dm], [M2])
        for h in range(2):
            hs = slice(h * 64, (h + 1) * 64)
            P.mm(pq[6][hs, :], X2[:, h * 64:(h + 1) * 64], M2[:, h * 128:(h + 1) * 128], [X2, M2], [pq[6]], start=True, stop=False)
            P.mm(pq[6][hs, :], Sbf[:, hs], M2[:, 256 + h * 128:256 + (h + 1) * 128], [Sbf, M2], [pq[6]], start=False, stop=True)
        P.mm(pq[7][:], X2[:, 256:384], X2[:, 128:256], [X2], [pq[7]])
        for h in range(2):
            hs = slice(h * 64, (h + 1) * 64)
            P.stt("dve", S32[:, hs], S32[:, hs], elast[:, j, h:h + 1], pq[7][:, hs], ALU.mult, ALU.add, [S32, elast, pq[7]], [S32])
        P.cp("act", Sbf[:], S32[:], [S32], [Sbf])
        P.stt("dve", yo[:, ts_], xc[:, ts_], V(V_M2D), pq[6][:], ALU.mult, ALU.add, [xc, vec, pq[6]], [yo])
    P.tt("dve", yo[:, N], yo[:, N], zd[:, N], ALU.mult, [yo, zd], [yo])
    store_out(c, 3, yo)


def rwkv_setup(c):
    P, S, V, vec = c["P"], c["S"], c["V"], c["vec"]
    lw = c["F"][0]
    P.dma("sp", lw[:, 0:128], c["lora_d"][:], writes=[lw])
    S["c_lwb"] = P.sb([128, 2, 128], BF16, "c_lwb")
    P.op("pool", lambda g: g.memset(S["c_lwb"][:], 0.0), [], [S["c_lwb"]])
    P.cp("dve", S["c_lwb"][0:64, 0, :], lw[0:64, 0:128], [lw], [S["c_lwb"]])
    P.cp("dve", S["c_lwb"][64:128, 1, :], lw[64:128, 0:128], [lw], [S["c_lwb"]])
    S["c_omm"] = P.sb([128, 4], F32, "c_omm")
    P.ts("dve", S["c_omm"][:], vec[:, V_MU:V_MU + 4], -1.0, 1.0, ALU.mult, ALU.add, [vec], [S["c_omm"]])
    S["c_H32"] = P.sb([128, 128], F32, "c_H32")
    S["c_Hbf"] = P.sb([128, 128], BF16, "c_Hbf")
    P.op("pool", lambda g: g.memset(S["c_H32"][:], 0.0), [], [S["c_H32"]])
    P.op("pool", lambda g: g.memset(S["c_Hbf"][:], 0.0), [], [S["c_Hbf"]])
    S["c_halo"] = P.sb([128, 4], F32, "c_halo")
    P.op("pool", lambda g: g.memset(S["c_halo"][:], 0.0), [], [S["c_halo"]])
    S["c_eps"] = P.sb([128, 1], F32, "c_eps")
    P.op("pool", lambda g: g.memset(S["c_eps"][:], 64e-5), [], [S["c_eps"]])
    for n in ("M", "MT"):
        S["c_" + n] = [P.sb([128, 512], F32, f"c_{n}{i}") for i in range(2)]
    S["c_Q"] = [P.sb([128, 256], F32, f"c_Q{i}") for i in range(2)]
    S["c_Zb"] = P.sb([128, 1024], BF16, "c_Zb")
    S["c_AT"] = P.sb([128, 2048], BF16, "c_AT")
    S["c_Aak"] = P.sb([128, 1024], BF16, "c_Aak")
    S["c_WT"] = P.sb([128, 64], BF16, "c_WT")
    S["c_Y1"] = P.sb([128, 128], BF16, "c_Y1")
    S["c_Ub"] = P.sb([128, 2, 128], BF16, "c_Ub")
    S["c_Vb"] = P.sb([128, 2, 128], BF16, "c_Vb")
    for n in ("c_Zb", "c_Y1", "c_Ub", "c_Vb"):
        P.op("pool", lambda g, b=S[n]: g.memset(b[:], 0.0), [], [S[n]])


def rwkv_block(c, blk):
    P, V, cst, S, pm, pq, F, H = c["P"], c["V"], c["cst"], c["S"], c["pm"], c["pq"], c["F"], c["H"]
    vec, evac, idb = c["vec"], c["evac"], c["idb"]
    pa, pt, pqb = c["pa"], c["pt"], c["pqb"]
    N = slice(0, TB)
    halo, omm, lwb = S["c_halo"], S["c_omm"], S["c_lwb"]
    raw = [F[0], F[1], F[2], F[3]]
    zc = F[4]
    r, k, v, wa = F[5], F[6], F[7], F[8]
    evac(5, raw[0], None, "dve", 1)
    evac(6, raw[1], None, "act", 1)
    evac(7, raw[2], None, "dve", 1)
    evac(8, raw[3], None, "act", 1)
    evac(9, zc, AF.Silu)
    for m, (src, dst) in enumerate(zip(raw, (r, k, v, wa))):
        P.cp("pool", src[:, 0:1], halo[:, m:m + 1], [halo], [src])
        P.cp("pool", halo[:, m:m + 1], src[:, TB:TB + 1], [src], [halo])
        P.ts("dve", dst[:, N], src[:, 1:TB + 1], omm[:, m:m + 1], None, ALU.mult, None, [src, omm], [dst])
        P.stt("dve", dst[:, N], src[:, 0:TB], V(V_MU + m), dst[:, N], ALU.mult, ALU.add, [src, vec, dst], [dst])
    lin = H[0]
    P.act(lin[0:64, N], wa[0:64, N], AF.Tanh, [wa], [lin])
    P.cp("dve", lin[64:128, N], wa[64:128, N], [wa], [lin])
    P.mm(pm[0][:], lwb[:, 0, :], lin[:, N], [lwb, lin], [pm[0]])
    P.mm(pm[1][:], lwb[:, 1, :], lin[:, N], [lwb, lin], [pm[1]])
    logw, iclr, kkn, km = F[0], F[1], F[2], F[3]
    P.act(logw[:, N], pm[0][:], AF.Sigmoid, [pm[0], vec], [logw], bias=V(V_W0))
    P.ts("dve", logw[:, N], logw[:, N], -0.606531, None, ALU.mult, None, [logw], [logw])
    P.act(iclr[:, N], pm[1][:], AF.Sigmoid, [pm[1], vec], [iclr], bias=V(V_A0))
    t9 = F[9]
    P.ts("dve", kkn[:, N], k[:, N], V(V_KK), None, ALU.mult, None, [k, vec], [kkn])
    P.tt("pool", t9[:, N], kkn[:, N], kkn[:, N], ALU.mult, [kkn], [t9])
    bones = cst[:, C_BONES:C_BONES + 128]
    P.mm(pm[0][:], bones, t9[:, N], [cst, t9], [pm[0]])
    P.act(t9[:, N], pm[0][:], AF.Sqrt, [pm[0]], [t9])
    P.ts("dve", t9[:, N], t9[:, N], 1e-12, None, ALU.max, None, [t9], [t9])
    P.op("dve", lambda g: g.reciprocal(t9[:, N], t9[:, N]), [t9], [t9])
    P.tt("dve", kkn[:, N], kkn[:, N], t9[:, N], ALU.mult, [kkn, t9], [kkn])
    P.ts("dve", km[:, N], iclr[:, N], -1.0, V(V_KA), ALU.add, ALU.mult, [iclr, vec], [km])
    P.stt("dve", km[:, N], km[:, N], 1.0, k[:, N], ALU.add, ALU.mult, [km, k], [km])
    cum, g_, gi, gp, E2, kia = F[9], F[10], F[11], F[12], F[13], F[14]
    m01 = cst[:, C_M01:C_M01 + TB]
    P.op("dve", lambda g: g.tensor_tensor_scan(cum[:, N], m01, logw[:, N], 0.0, ALU.mult, ALU.add), [cst, logw], [cum])
    P.act(g_[:, N], cum[:, N], AF.Exp, [cum], [g_])
    P.act(gi[:, N], cum[:, N], AF.Exp, [cum], [gi], scale=-1.0)
    P.tt("pool", gp[:, N], cum[:, N], logw[:, N], ALU.subtract, [cum, logw], [gp])
    P.act(gp[:, N], gp[:, N], AF.Exp, [gp], [gp])

    def v3(buf, off=0, w=64):
        return buf.t[:, off:off + 8 * w].rearrange("p (c s) -> p c s", s=w)

    P.tt("dve", v3(E2), v3(gi), v3(g_)[:, :, 63:64].broadcast_to([128, 8, 64]), ALU.mult, [gi, g_], [E2])
    RA, BK, BKh, vb = H[1], H[2], H[3], H[4]
    ra3 = v3(RA, 0, 128)
    P.tt("dve", ra3[:, :, 0:64], v3(r), v3(g_), ALU.mult, [r, g_], [RA])
    P.stt("dve", ra3[:, :, 64:128], v3(kkn), -1.0, v3(gp), ALU.mult, ALU.mult, [kkn, gp], [RA])
    P.tt("pool", kia[:, N], kkn[:, N], iclr[:, N], ALU.mult, [kkn, iclr], [kia])
    P.tt("dve", BK[:, 0:TB], kia[:, N], gi[:, N], ALU.mult, [kia, gi], [BK])
    P.tt("dve", BK[:, TB:2 * TB], km[:, N], gi[:, N], ALU.mult, [km, gi], [BK])
    BK1 = H[8]
    P.ts("dve", BK1[:, :], BK[:, :], cst[:, C_BONES + 64:C_BONES + 65], None, ALU.mult, None, [BK, cst], [BK1])
    P.ts("dve", BK[:, :], BK[:, :], cst[:, C_BONES:C_BONES + 1], None, ALU.mult, None, [BK, cst], [BK])
    BKm = (BK, BK1)
    P.tt("pool", BKh[:, 0:TB], kia[:, N], E2[:, N], ALU.mult, [kia, E2], [BKh])
    P.tt("pool", BKh[:, TB:2 * TB], km[:, N], E2[:, N], ALU.mult, [km, E2], [BKh])
    P.cp("act", vb[:, N], v[:, N], [v], [vb])
    Atok, Btok, Ktok, Vtok = H[5], H[6], H[7], H[0]
    srcs = ((lambda cc: ra3[:, cc, 64:128], RA, Atok), (lambda cc: BKh[:, cc * 64:(cc + 1) * 64], BKh, Btok),
            (lambda cc: BKh[:, TB + cc * 64:TB + (cc + 1) * 64], BKh, Ktok), (lambda cc: vb[:, cc * 64:(cc + 1) * 64], vb, Vtok))
    n = 0
    for (fsrc, sbuf, dst) in srcs:
        for half in range(2):
            pp = pm[n % 2]
            n += 1
            fns = [lambda g, cc=cc, fsrc=fsrc, pp=pp, half=half: g.matmul(pp[0:64, (cc - 4 * half) * 128:(cc - 4 * half + 1) * 128],
                                                                         fsrc(cc), idb[:], start=True, stop=True)
                   for cc in range(4 * half, 4 * half + 4)]
            P.group("pe", fns, [sbuf, idb], [pp])
            P.cp("act" if half else "dve", dst[0:64, half * 512:(half + 1) * 512], pp[0:64, :], [pp], [dst])
    import os
    stage = float(os.environ.get("RSTAGE", "9"))
    if stage < 0.5:
        return
    AT, Aak, Zb = S["c_AT"], S["c_Aak"], S["c_Zb"]
    MI = cst[0:64, C_MH:C_MH + 512]
    MS = cst[0:64, C_MR:C_MR + 256]
    ML = cst[0:64, C_MR + 512:C_MR + 768]
    I4 = cst[0:64, C_ID:C_ID + 64].unsqueeze(1).broadcast_to([64, 4, 64])
    for pair in range(2):
        bts = (2 * pair, 2 * pair + 1)
        for bi, bt in enumerate(bts):
            Mk, MkT, Q = S["c_M"][bi], S["c_MT"][bi], S["c_Q"][bi]
            fns = []
            for q4 in range(4):
                cc, h = 2 * bt + q4 // 2, q4 % 2
                ph = slice(h * 64, (h + 1) * 64)
                cs_ = slice(cc * 64, (cc + 1) * 64)
                bt_, kt_ = BKm[h][:, cs_], BKm[h][:, TB + cc * 64:TB + (cc + 1) * 64]
                rt_, at_ = ra3[:, cc, 0:64], ra3[:, cc, 64:128]
                o = q4 * 64
                fns.append(lambda g, a=bt_, b=rt_, o=o: g.matmul(pm[0][0:64, o:o + 64], a, b, start=True, stop=True))
                fns.append(lambda g, a=kt_, b=rt_, o=o: g.matmul(pm[0][0:64, 256 + o:256 + o + 64], a, b, start=True, stop=True))
                fns.append(lambda g, a=bt_, b=at_, o=o: g.matmul(pm[1][0:64, o:o + 64], a, b, start=True, stop=True))
                fns.append(lambda g, a=kt_, b=at_, o=o: g.matmul(pm[1][0:64, 256 + o:256 + o + 64], a, b, start=True, stop=True))
                fns.append(lambda g, a=at_, b=bt_, o=o, bi=bi: g.matmul(pa[bi][0:64, o:o + 64], a, b, start=True, stop=True))
            P.group("pe", fns, [BK, BK1, RA], [pm[0], pm[1], pa[bi]])
            rsub = int(os.environ.get("RSUB", "9"))
            if rsub >= 2:
                P.tt("dve", AT[0:64, bt * 512:(bt + 1) * 512], pm[0][0:64, :], MI, ALU.mult, [pm[0], cst], [AT])
            if rsub >= 3:
                P.tt("dve", Mk[0:64, 0:256], pm[1][0:64, 0:256], MS, ALU.mult, [pm[1], cst], [Mk])
            if rsub >= 4:
                P.tt("dve", Aak[0:64, bt * 256:(bt + 1) * 256], pm[1][0:64, 256:512], MS, ALU.mult, [pm[1], cst], [Aak])
            if rsub >= 5:
                P.tt("dve", MkT[0:64, 0:256], pa[bi][0:64, 0:256], ML, ALU.mult, [pa[bi], cst], [MkT])
            if rsub >= 6:
                P.tt("dve", Q[0:64, 0:256], Mk[0:64, 0:256], cst[0:64, C_I4:C_I4 + 256], ALU.add, [Mk, cst], [Q])
        for rnd in range(6 if stage >= 1 else (1 if stage >= 0.7 else 0)):
            for bi, bt in enumerate(bts):
                Mk, MkT, Q = S["c_M"][bi], S["c_MT"][bi], S["c_Q"][bi]
                cur, nxt = (rnd % 2) * 256, ((rnd + 1) % 2) * 256
                fns = []
                for q4 in range(4):
                    o = q4 * 64
                    m_, mt_ = Mk[0:64, cur + o:cur + o + 64], MkT[0:64, cur + o:cur + o + 64]
                    if rnd < 4:
                        fns.append(lambda g, a=mt_, b=m_, o=o, bi=bi: g.matmul(pa[bi][0:64, o:o + 64], a, b, start=True, stop=True))
                    if rnd < 5:
                        fns.append(lambda g, a=m_, b=mt_, o=o, bi=bi: g.matmul(pa[bi][0:64, 256 + o:256 + o + 64], a, b, start=True, stop=True))
                    if rnd > 0:
                        fns.append(lambda g, a=mt_, o=o, bi=bi, Q=Q: g.matmul(pt[bi][0:64, o:o + 64], a, Q[0:64, o:o + 64], start=True, stop=True))
                P.group("pe", fns, [Mk, MkT, Q], [pa[bi], pt[bi]])
            for bi, bt in enumerate(bts):
                Mk, MkT, Q = S["c_M"][bi], S["c_MT"][bi], S["c_Q"][bi]
                cur, nxt = (rnd % 2) * 256, ((rnd + 1) % 2) * 256
                if rnd > 0:
                    P.tt("dve", Q[0:64, 0:256], Q[0:64, 0:256], pt[bi][0:64, 0:256], ALU.add, [Q, pt[bi]], [Q])
                if rnd < 4:
                    P.cp("act", Mk[0:64, nxt:nxt + 256], pa[bi][0:64, 0:256], [pa[bi]], [Mk])
                if rnd < 5:
                    P.cp("act", MkT[0:64, nxt:nxt + 256], pa[bi][0:64, 256:512], [pa[bi]], [MkT])
        for bi, bt in enumerate(bts):
            P.cp("dve", Zb[0:64, bt * 256:(bt + 1) * 256], S["c_Q"][bi][0:64, 0:256], [S["c_Q"][bi]], [Zb])
    if stage < 2:
        return
    H32, Hbf, WT, Y1, Ub, Vb = S["c_H32"], S["c_Hbf"], S["c_WT"], S["c_Y1"], S["c_Ub"], S["c_Vb"]
    oT = pqb[1]
    OB = pq[4:8]
    for cc in range(8):
        cs_ = slice(cc * 64, (cc + 1) * 64)
        ck = slice(cc * 128, (cc + 1) * 128)
        P.mm(pq[0][:, :], Atok[0:64, ck], Zb[0:64, cc * 128:(cc + 1) * 128], [Atok, Zb], [pq[0]])
        for h in range(2):
            ph = slice(h * 64, (h + 1) * 64)
            q = cc * 2 + h
            P.mm(pq[1][0:64, ph], Aak[0:64, q * 64:(q + 1) * 64], Vtok[0:64, cc * 128 + h * 64:cc * 128 + (h + 1) * 64], [Aak, Vtok], [pq[1]])
            P.cp("act" if h else "dve", WT[ph, :], pq[0][ph, ph], [pq[0]], [WT])
            P.cp("pool", Vb[0:64, h, ph], Vtok[0:64, cc * 128 + h * 64:cc * 128 + (h + 1) * 64], [Vtok], [Vb])
        P.cp("dve", Y1[0:64, :], pq[1][0:64, :], [pq[1]], [Y1])
        for h in range(2):
            ph = slice(h * 64, (h + 1) * 64)
            q = cc * 2 + h
            P.mm(pq[2][0:64, ph], Zb[:, q * 64:(q + 1) * 64], Y1[:, ph], [Zb, Y1], [pq[2]], start=True, stop=False)
            P.mm(pq[2][0:64, ph], WT[:, :], Hbf[:, ph], [WT, Hbf], [pq[2]], start=False, stop=True)
        for h in range(2):
            ph = slice(h * 64, (h + 1) * 64)
            P.cp("act" if h else "dve", Ub[0:64, h, ph], pq[2][0:64, ph], [pq[2]], [Ub])
        P.mm(oT[:, cs_], Hbf[:, :], ra3[:, cc, 0:64], [Hbf, RA], OB, start=True, stop=False)
        for h in range(2):
            q = cc * 2 + h
            ao = (cc // 2) * 512 + (q % 4) * 64
            P.mm(oT[:, cs_], Ub[0:64, h, :], AT[0:64, ao:ao + 64], [Ub, AT], OB, start=False, stop=False)
            P.mm(oT[:, cs_], Vb[0:64, h, :], AT[0:64, 256 + ao:256 + ao + 64], [Vb, AT], OB, start=False, stop=(h == 1))
        for h in range(2):
            P.mm(pq[3][:, :], Btok[0:64, ck], Ub[0:64, h, :], [Btok, Ub], [pq[3]], start=(h == 0), stop=False)
        for h in range(2):
            P.mm(pq[3][:, :], Ktok[0:64, ck], Vb[0:64, h, :], [Ktok, Vb], [pq[3]], start=False, stop=(h == 1))
        for h in range(2):
            ph = slice(h * 64, (h + 1) * 64)
            P.stt("dve", H32[ph, ph], H32[ph, ph], g_[ph, cc * 64 + 63:cc * 64 + 64], pq[3][ph, ph], ALU.mult, ALU.add, [H32, g_, pq[3]], [H32])
        P.cp("act", Hbf[:], H32[:], [H32], [Hbf])
    y, yc, sq, rk = F[9], F[10], F[11], F[12]
    P.cp("act", y[:, N], oT[:], OB, [y])
    P.mm(pm[0][:], bones, y[:, N], [cst, y], [pm[0]])
    P.stt("dve", yc[:, N], pm[0][:], -1.0 / 64, y[:, N], ALU.mult, ALU.add, [pm[0], y], [yc])
    P.tt("pool", sq[:, N], yc[:, N], yc[:, N], ALU.mult, [yc], [sq])
    P.mm(pm[1][:], bones, sq[:, N], [cst, sq], [pm[1]])
    P.act(sq[:, N], pm[1][:], AF.Sqrt, [pm[1], S["c_eps"]], [sq], bias=S["c_eps"][:], scale=1.0 / 64)
    P.op("dve", lambda g: g.reciprocal(sq[:, N], sq[:, N]), [sq], [sq])
    P.tt("dve", yc[:, N], yc[:, N], sq[:, N], ALU.mult, [yc, sq], [yc])
    P.ts("dve", yc[:, N], yc[:, N], V(V_LNW), V(V_LNB), ALU.mult, ALU.add, [yc, vec], [yc])
    P.stt("dve", rk[:, N], r[:, N], V(V_RK), km[:, N], ALU.mult, ALU.mult, [r, vec, km], [rk])
    P.mm(pm[0][:], bones, rk[:, N], [cst, rk], [pm[0]])
    P.tt("dve", rk[:, N], pm[0][:], v[:, N], ALU.mult, [pm[0], v], [rk])
    P.tt("dve", yc[:, N], yc[:, N], rk[:, N], ALU.add, [yc, rk], [yc])
    P.tt("dve", yc[:, N], yc[:, N], zc[:, N], ALU.mult, [yc, zc], [yc])
    store_out(c, 2, yc)


NV2 = 80


def prep_phase2(inp, l):
    w_in = inp["w_in"][l]
    off_zb = 5 * W
    off_gate = w_in.shape[1] - 4 * D_MODEL
    wzb = np.ascontiguousarray(w_in[:, off_zb:off_zb + W].reshape(16, 128, 2, 512).transpose(2, 1, 0, 3)).reshape(2, 128, 8192)
    wgate = np.ascontiguousarray(w_in[:, off_gate:].reshape(16, 128, 4, 16, 128).transpose(3, 1, 0, 2, 4)).reshape(16, 128, 8192)
    wup = np.ascontiguousarray(inp["w_up"][l].reshape(4, 8, 128, 16, 128).transpose(3, 2, 0, 1, 4)).reshape(16, 128, 4096)
    wglu = np.ascontiguousarray(inp["s5_w_glu"][l].reshape(8, 128, 1024).transpose(1, 0, 2)).reshape(128, 8192)
    wout = np.ascontiguousarray(inp["w_out"][l].reshape(16, 128, 4, 512).transpose(2, 1, 0, 3)).reshape(4, 128, 8192)
    vec = np.zeros((128, NV2), np.float32)
    vec[:, 0:64] = inp["gate_bias"][l].reshape(4, 16, 128).transpose(2, 0, 1).reshape(128, 64)
    vec[:, 64:72] = inp["s5_b_glu"][l].reshape(8, 128).T
    vec[:, 72:80] = inp["m2_norm"][l].reshape(8, 128).T
    rows = np.ascontiguousarray(np.stack([inp["norm_pre"][l], inp["norm_post"][l]], 0))
    return dict(wzb=wzb, wgate=wgate, wup=wup, wglu=wglu, wout=wout, vec2=vec, rows2=rows)


def build_phase2(Tc):
    NB = Tc // TB
    nc = bass.Bass("TRN2", target_bir_lowering=False)
    with ExitStack() as es:
        P = Prog(nc, es)
        x_d = P.dram("xres", [Tc, D_MODEL], F32, "ExternalInput")
        bo_d = P.dram("bo", [4, W, Tc], F32, "ExternalInput")
        wzb_d = P.dram("wzb", [2, 128, 8192], F32, "ExternalInput")
        wgate_d = P.dram("wgate", [16, 128, 8192], F32, "ExternalInput")
        wup_d = P.dram("wup", [16, 128, 4096], F32, "ExternalInput")
        wglu_d = P.dram("wglu", [128, 8192], F32, "ExternalInput")
        wout_d = P.dram("wout", [4, 128, 8192], F32, "ExternalInput")
        vec_d = P.dram("vec2", [128, NV2], F32, "ExternalInput")
        row_d = P.dram("rows2", [2, D_MODEL], F32, "ExternalInput")
        cst_d = P.dram("consts", [128, NCST], F32, "ExternalInput")
        out_d = P.dram("res_out", [Tc, D_MODEL], F32, "ExternalOutput")

        vec = P.sb([128, NV2], F32, "vec")
        gb = P.sb([128, D_MODEL], F32, "gb")
        npb = P.sb([128, D_MODEL], F32, "npb")
        idf = P.sb([128, 128], F32, "idf")
        idb = P.sb([128, 128], BF16, "idb")
        onesb = P.sb([128, 128], BF16, "onesb")
        P.dma("sp", vec[:], vec_d[:], writes=[vec])
        P.dma("sp", gb[:], row_d.t[0:1, :].partition_broadcast(128), writes=[gb])
        P.dma("sp", npb[:], row_d.t[1:2, :].partition_broadcast(128), writes=[npb])
        P.dma("sp", idf[:], cst_d[:, C_ID:C_ID + 128], writes=[idf])
        P.cp("dve", idb[:], idf[:], [idf], [idb])
        P.op("pool", lambda g: g.memset(onesb[:], 1.0), [], [onesb])
        eps6 = P.sb([128, 1], F32, "eps6")
        eps5 = P.sb([128, 1], F32, "eps5")
        P.op("pool", lambda g: g.memset(eps6[:], 1e-6), [], [eps6])
        P.op("pool", lambda g: g.memset(eps5[:], 1e-5), [], [eps5])

        xt = [P.sb([128, D_MODEL], F32, f"xt{i}") for i in range(2)]
        xn = P.sb([128, D_MODEL], BF16, "xn")
        hT = P.sb([128, 16, TB], BF16, "hT")
        ss = P.sb([128, 1], F32, "ss")
        rstd = P.sb([128, 1], F32, "rstd")
        BO = [P.sb([128, 8, TB], BF16, f"BO{b}") for b in range(4)]
        zbs = P.sb([128, 8, TB], BF16, "zbs")
        wch = [P.sb([128, 8192], BF16, f"wch{i}") for i in range(2)]
        wupb = [P.sb([128, 4096], BF16, f"wupb{i}") for i in range(2)]
        mT = P.sb([128, 16, TB], BF16, "mT")
        Y = [P.sb([128, D_MODEL], F32, f"Y{i}") for i in range(4)]
        G = [P.sb([128, TB], F32, f"G{i}") for i in range(2)]
        acc = P.sb([128, TB], F32, "acc")
        tmp = P.sb([128, TB], F32, "tmp")
        sqb = P.sb([128, TB], BF16, "sqb")

        pt = [P.ps([128, 512], F32, f"pt{i}") for i in range(2)]
        pg = [P.ps([128, 512], F32, f"pg{i}") for i in range(2)]
        pu = [P.ps([128, 512], F32, f"pu{i}") for i in range(2)]
        pz = P.ps([128, 512], F32, "pz")

        nw = [0]

        def load_w(src_ap, n=8192):
            b = wch[nw[0] % 2]
            nw[0] += 1
            P.dma("pool", b[:, 0:n], src_ap, writes=[b])
            return b

        for blk in range(NB):
            t0 = blk * TB
            for j in range(4):
                x = xt[j % 2]
                P.dma("sp", x[:], x_d[t0 + j * 128:t0 + (j + 1) * 128, :], writes=[x])
                P.op("dve", lambda g: g.memset(ss[:], 0.0), [], [ss])
                P.act(xn[:], x[:], AF.Square, [x, ss], [xn, ss], accum=ss[:])
                P.act(rstd[:], ss[:], AF.Sqrt, [ss, eps6], [rstd], bias=eps6[:], scale=1.0 / D_MODEL)
                P.op("dve", lambda g: g.reciprocal(rstd[:], rstd[:]), [rstd], [rstd])
                P.stt("dve", xn[:], x[:], rstd[:], gb[:], ALU.mult, ALU.mult, [x, rstd, gb], [xn])
                for k4 in range(4):
                    p = pt[k4 % 2]
                    fns = [lambda g, kc=kc, p=p, k4=k4: g.matmul(p[:, (kc - 4 * k4) * 128:(kc - 4 * k4 + 1) * 128],
                                                                 xn[:, kc * 128:(kc + 1) * 128], idb[:], start=True, stop=True)
                           for kc in range(4 * k4, 4 * k4 + 4)]
                    P.group("pe", fns, [xn, idb], [p])
                    P.cp("act" if k4 % 2 else "dve", hT[:, 4 * k4:4 * k4 + 4, j * 128:(j + 1) * 128],
                         p[:].rearrange("p (a b) -> p a b", b=128), [p], [hT])
            for b in range(4):
                P.dma("pool", BO[b][:], bo_d.t[b, :, t0:t0 + TB].rearrange("(c p) t -> p c t", p=128), writes=[BO[b]])
            for half in range(2):
                wb = load_w(wzb_d.t[half])
                w3 = wb.t[:, :].rearrange("p (k n) -> p k n", n=512)
                for cj in range(4):
                    c8 = half * 4 + cj
                    p = pg[c8 % 2]
                    fns = [lambda g, kc=kc, p=p, cj=cj, w3=w3: g.matmul(p[:], w3[:, kc, cj * 128:(cj + 1) * 128], hT[:, kc, :],
                                                                        start=(kc == 0), stop=(kc == 15)) for kc in range(16)]
                    P.group("pe", fns, [wb, hT], [p])
                    P.act(G[c8 % 2][:], p[:], AF.Silu, [p], [G[c8 % 2]])
                    P.tt("dve", zbs[:, c8, :], G[c8 % 2][:], BO[1][:, c8, :], ALU.mult, [G[c8 % 2], BO[1]], [zbs])
            wb = load_w(wglu_d[:])
            w3 = wb.t[:, :].rearrange("p (k n) -> p k n", n=1024)
            for c8 in range(8):
                p = pg[c8 % 2]
                fns = [lambda g, kc=kc, p=p, c8=c8, w3=w3: g.matmul(p[:], w3[:, kc, c8 * 128:(c8 + 1) * 128], BO[1][:, kc, :],
                                                                    start=(kc == 0), stop=(kc == 7)) for kc in range(8)]
                P.group("pe", fns, [wb, BO[1]], [p])
                P.act(G[c8 % 2][:], p[:], AF.Sigmoid, [p, vec], [G[c8 % 2]], bias=vec[:, 64 + c8:65 + c8])
                P.tt("dve", zbs[:, c8, :], zbs[:, c8, :], G[c8 % 2][:], ALU.mult, [zbs, G[c8 % 2]], [zbs])
            for grp in range(4):
                for k, cc in enumerate((2 * grp, 2 * grp + 1)):
                    P.tt("pool", sqb[:], BO[3][:, cc, :], BO[3][:, cc, :], ALU.mult, [BO[3]], [sqb])
                    P.mm(pz[:], onesb[:], sqb[:], [onesb, sqb], [pz], start=(k == 0), stop=(k == 1))
                P.act(tmp[:], pz[:], AF.Sqrt, [pz, eps5], [tmp], bias=eps5[:], scale=1.0 / 256)
                P.op("dve", lambda g: g.reciprocal(tmp[:], tmp[:]), [tmp], [tmp])
                for cc in (2 * grp, 2 * grp + 1):
                    P.stt("dve", BO[3][:, cc, :], BO[3][:, cc, :], vec[:, 72 + cc:73 + cc], tmp[:], ALU.mult, ALU.mult,
                          [BO[3], vec, tmp], [BO[3]])
            SRC = (BO[0], zbs, BO[2], BO[3])
            for j in range(16):
                wb = load_w(wgate_d.t[j])
                w3 = wb.t[:, :].rearrange("p (k n) -> p k n", n=512)
                ub = wupb[j % 2]
                P.dma("pool", ub[:], wup_d.t[j], writes=[ub])
                u3 = ub.t[:, :].rearrange("p (k n) -> p k n", n=128)
                for b in range(4):
                    p1, p2 = pg[b % 2], pu[b % 2]
                    fns = [lambda g, kc=kc, p1=p1, b=b, w3=w3: g.matmul(p1[:], w3[:, kc, b * 128:(b + 1) * 128], hT[:, kc, :],
                                                                        start=(kc == 0), stop=(kc == 15)) for kc in range(16)]
                    P.group("pe", fns, [wb, hT], [p1])
                    fns = [lambda g, kc=kc, p2=p2, b=b, u3=u3: g.matmul(p2[:], u3[:, b * 8 + kc, :], SRC[b][:, kc, :],
                                                                        start=(kc == 0), stop=(kc == 7)) for kc in range(8)]
                    P.group("pe", fns, [ub, SRC[b]], [p2])
                    P.act(G[b % 2][:], p1[:], AF.Sigmoid, [p1, vec], [G[b % 2]], bias=vec[:, b * 16 + j:b * 16 + j + 1])
                    if b == 0:
                        P.tt("dve", acc[:], p2[:], G[b % 2][:], ALU.mult, [p2, G[b % 2]], [acc])
                    else:
                        P.tt("dve", tmp[:], p2[:], G[b % 2][:], ALU.mult, [p2, G[b % 2]], [tmp])
                        if b < 3:
                            P.tt("pool", acc[:], acc[:], tmp[:], ALU.add, [acc, tmp], [acc])
                        else:
                            P.tt("dve", mT[:, j, :], acc[:], tmp[:], ALU.add, [acc, tmp], [mT])
            for ec in range(4):
                wb = load_w(wout_d.t[ec])
                w3 = wb.t[:, :].rearrange("p (k n) -> p k n", n=512)
                for tt_ in range(4):
                    p = pu[tt_ % 2]
                    fns = [lambda g, kc=kc, p=p, tt_=tt_, w3=w3: g.matmul(p[:], mT[:, kc, tt_ * 128:(tt_ + 1) * 128], w3[:, kc, :],
                                                                          start=(kc == 0), stop=(kc == 15)) for kc in range(16)]
                    P.group("pe", fns, [wb, mT], [p])
                    P.cp("act" if tt_ % 2 else "dve", Y[tt_][:, ec * 512:(ec + 1) * 512], p[:], [p], [Y[tt_]])
            for tt_ in range(4):
                x = xt[tt_ % 2]
                P.dma("sp", x[:], x_d[t0 + tt_ * 128:t0 + (tt_ + 1) * 128, :], writes=[x])
                P.op("dve", lambda g: g.memset(ss[:], 0.0), [], [ss])
                P.act(xn[:], Y[tt_][:], AF.Square, [Y[tt_], ss], [xn, ss], accum=ss[:])
                P.act(rstd[:], ss[:], AF.Sqrt, [ss, eps6], [rstd], bias=eps6[:], scale=1.0 / D_MODEL)
                P.op("dve", lambda g: g.reciprocal(rstd[:], rstd[:]), [rstd], [rstd])
                P.stt("dve", Y[tt_][:], Y[tt_][:], rstd[:], npb[:], ALU.mult, ALU.mult, [Y[tt_], rstd, npb], [Y[tt_]])
                P.tt("pool", Y[tt_][:], Y[tt_][:], x[:], ALU.add, [Y[tt_], x], [Y[tt_]])
                P.dma("sp", out_d[t0 + tt_ * 128:t0 + (tt_ + 1) * 128, :], Y[tt_][:], reads=[Y[tt_]], writes=[out_d], key="d_out")
        P.finish([out_d])
        print("phase2: ninst", P.ninst, "sbuf bytes/partition", P.sbytes)
    return nc


_PROGS = {}


def kernel(**inp):
    inp = {k: np.asarray(v) for k, v in inp.items()}
    x = inp["x"]
    T = x.shape[1]
    NCORE = 8
    Tc = T // NCORE
    cst = make_consts()
    if ("p1", T) not in _PROGS:
        _PROGS[("p1", T)] = build_phase1(T)
    if ("p2", Tc) not in _PROGS:
        _PROGS[("p2", Tc)] = build_phase2(Tc)
    res = np.ascontiguousarray(x[0], dtype=np.float32)
    for l in range(2):
        in_maps = []
        for c in range(NCORE):
            d = prep_phase1(inp, l, c)
            d["xres"] = res
            d["consts"] = cst
            in_maps.append(d)
        r1 = run_bass_kernel_spmd(_PROGS[("p1", T)], in_maps, core_ids=list(range(NCORE)))
        bo = np.concatenate([r1.results[c]["outs"] for c in range(NCORE)], axis=1)
        d2 = prep_phase2(inp, l)
        in_maps = []
        for c in range(NCORE):
            d = dict(d2)
            d["xres"] = np.ascontiguousarray(res[c * Tc:(c + 1) * Tc])
            d["bo"] = np.ascontiguousarray(bo[:, :, c * Tc:(c + 1) * Tc])
            d["consts"] = cst
            in_maps.append(d)
        r2 = run_bass_kernel_spmd(_PROGS[("p2", Tc)], in_maps, core_ids=list(range(NCORE)))
        res = np.ascontiguousarray(np.concatenate([r2.results[c]["res_out"] for c in range(NCORE)], axis=0))
    return res[None].astype(np.float32)
```

```python
import numpy as np
from contextlib import ExitStack
import concourse.bass as bass
import concourse.mybir as mybir
from concourse.bass_utils import run_bass_kernel_spmd

F32 = mybir.dt.float32
BF16 = mybir.dt.bfloat16
I32 = mybir.dt.int32
ALU = mybir.AluOpType
AF = mybir.ActivationFunctionType

D_MODEL = 2048
W = 1024
TB = 512
NW1 = 14 * 128 + 2
PI = 3.14159265358979


class Buf:
    def __init__(self, t, name):
        self.t = t
        self.name = name
        self.w = None
        self.r = {}

    def __getitem__(self, k):
        return self.t[k]


class Prog:
    ENGS = ["pe", "act", "dve", "pool", "sp"]

    def __init__(self, nc, es):
        self.nc = nc
        self.es = es
        self.q = {e: [] for e in self.ENGS}
        self.semobj = {}
        for e in self.ENGS:
            self.semobj[e] = es.enter_context(nc.semaphore("s_" + e))
        self.cnt = {e: 0 for e in self.ENGS}
        self.seen = {e: {} for e in self.ENGS}
        self.dcnt = {}
        self.nbuf = 0
        self.ninst = 0
        self.sbytes = 0

    def sb(self, shape, dt=F32, name=None):
        self.nbuf += 1
        name = name or f"sb{self.nbuf}"
        t = self.es.enter_context(self.nc.sbuf_tensor(name, list(shape), dt))
        n = 1
        for s in shape[1:]:
            n *= s
        self.sbytes += n * (4 if dt in (F32, I32) else 2)
        return Buf(t, name)

    def ps(self, shape, dt=F32, name=None):
        self.nbuf += 1
        name = name or f"ps{self.nbuf}"
        t = self.es.enter_context(self.nc.psum_tensor(name, list(shape), dt))
        return Buf(t, name)

    def dram(self, name, shape, dt, kind):
        t = self.nc.dram_tensor(name, list(shape), dt, kind=kind)
        return Buf(t.ap(), name)

    def _waits(self, e, reads, writes):
        need = {}

        def add(ev, raw):
            if ev is None:
                return
            k, v = ev
            if k == e:
                if e == "pe":
                    return
            if need.get(k, 0) < v:
                need[k] = v

        reads = [getattr(b, "base", b) for b in reads]
        writes = [getattr(b, "base", b) for b in writes]
        for b in reads:
            add(b.w, True)
        for b in writes:
            add(b.w, False)
            for k, v in b.r.items():
                add((k, v), False)
        for k, v in need.items():
            if self.seen[e].get(k, 0) >= v:
                continue
            self.seen[e][k] = v
            s = self.semobj[k]
            self.q[e].append(lambda eng, s=s, v=v: eng.wait_ge(s, v))
            self.ninst += 1

    def _record(self, ev, reads, writes):
        k, v = ev
        reads = [getattr(b, "base", b) for b in reads]
        writes = [getattr(b, "base", b) for b in writes]
        for b in reads:
            if b.r.get(k, 0) < v:
                b.r[k] = v
        for b in writes:
            b.w = ev
            b.r = {}

    def op(self, e, fn, reads=(), writes=()):
        self._waits(e, reads, writes)
        self.cnt[e] += 1
        s = self.semobj[e]
        self.q[e].append(lambda eng, fn=fn, s=s: fn(eng).then_inc(s, 1))
        self.ninst += 1
        self._record((e, self.cnt[e]), reads, writes)

    def group(self, e, fns, reads=(), writes=()):
        self._waits(e, reads, writes)
        self.cnt[e] += 1
        s = self.semobj[e]
        for f in fns[:-1]:
            self.q[e].append(lambda eng, f=f: f(eng))
        self.q[e].append(lambda eng, f=fns[-1], s=s: f(eng).then_inc(s, 1))
        self.ninst += len(fns)
        self._record((e, self.cnt[e]), reads, writes)

    def dma(self, e, out_ap, in_ap, reads=(), writes=(), key=None):
        if key is None:
            key = "d_" + (writes[0].name if writes else reads[0].name)
        if key not in self.semobj:
            self.semobj[key] = self.es.enter_context(self.nc.semaphore(key))
            self.dcnt[key] = 0
        self._waits(e, reads, writes)
        self.dcnt[key] += 16
        s = self.semobj[key]
        self.q[e].append(lambda eng, o=out_ap, i=in_ap, s=s: eng.dma_start(out=o, in_=i).then_inc(s, 16))
        self.ninst += 1
        self._record((key, self.dcnt[key]), reads, writes)

    def finish(self, out_bufs):
        for b in out_bufs:
            self._waits("sp", [b], [b])
        for k, v in self.dcnt.items():
            if self.seen["sp"].get(k, 0) < v:
                self.seen["sp"][k] = v
                self.q["sp"].append(lambda eng, s=self.semobj[k], v=v: eng.wait_ge(s, v))
        nc = self.nc
        with nc.Block() as block:

            @block.tensor
            def _(eng):
                for f in self.q["pe"]:
                    f(eng)

            @block.scalar
            def _(eng):
                for f in self.q["act"]:
                    f(eng)

            @block.vector
            def _(eng):
                for f in self.q["dve"]:
                    f(eng)

            @block.gpsimd
            def _(eng):
                for f in self.q["pool"]:
                    f(eng)

            @block.sync
            def _(eng):
                for f in self.q["sp"]:
                    f(eng)

    def tt(self, e, out, a, b, op, R, Wr):
        self.op(e, lambda g: g.tensor_tensor(out, a, b, op), R, Wr)

    def ts(self, e, out, a, s1, s2, op0, op1, R, Wr):
        if s2 is None:
            self.op(e, lambda g: g.tensor_scalar(out, a, s1, None, op0), R, Wr)
        else:
            self.op(e, lambda g: g.tensor_scalar(out, a, s1, s2, op0, op1), R, Wr)

    def stt(self, e, out, a, sc, b, op0, op1, R, Wr):
        self.op(e, lambda g: g.scalar_tensor_tensor(out, a, sc, b, op0, op1), R, Wr)

    def act(self, out, a, func, R, Wr, bias=None, scale=None, accum=None):
        kw = {}
        if bias is not None:
            kw["bias"] = bias
        if scale is not None:
            kw["scale"] = scale
        if accum is not None:
            kw["accum_out"] = accum
        self.op("act", lambda g: g.activation(out=out, in_=a, func=func, **kw), R, Wr)

    def cp(self, e, out, a, R, Wr):
        if e == "act":
            self.op(e, lambda g: g.copy(out, a), R, Wr)
        else:
            self.op(e, lambda g: g.tensor_copy(out, a), R, Wr)

    def fence(self, e):
        v = self.cnt[e]
        if v > 0 and self.seen[e].get(e, 0) < v:
            self.seen[e][e] = v
            self.q[e].append(lambda eng, s=self.semobj[e], v=v: eng.wait_ge(s, v))

    def mm(self, out, lhsT, rhs, R, Wr, start=True, stop=True, fence=False):
        if fence:
            self.fence("pe")
        self.op("pe", lambda g: g.matmul(out, lhsT, rhs, start=start, stop=stop), R, Wr)


C_ID, C_TRI, C_NEG, C_BONES, C_MH, C_MR, C_M01, C_IOTA, C_ONES, C_I4, NCST = 0, 128, 256, 384, 512, 1024, 2048, 2560, 3072, 3200, 3456


def make_consts():
    c = np.zeros((128, NCST), np.float32)
    i = np.arange(128)
    c[:, C_ID:C_ID + 128] = np.eye(128)
    c[:, C_TRI:C_TRI + 128] = (i[:, None] <= i[None, :])
    c[:, C_NEG:C_NEG + 128] = np.where(i[None, :] >= i[:, None], 0.0, -1e5)
    c[:, C_BONES:C_BONES + 128] = ((i[:, None] // 64) == (i[None, :] // 64))
    j = np.arange(64)
    incl = (j[:, None] <= j[None, :]).astype(np.float32)
    strict = (j[:, None] < j[None, :]).astype(np.float32)
    c[0:64, C_MH:C_MH + 512] = np.tile(incl, (1, 8))
    c[0:64, C_MR:C_MR + 512] = np.tile(strict, (1, 8))
    c[0:64, C_MR + 512:C_MR + 1024] = np.tile(strict.T, (1, 8))
    t = np.arange(512)
    c[:, C_M01:C_M01 + 512] = (t % 64 != 0)[None, :]
    c[:, C_IOTA:C_IOTA + 512] = t[None, :]
    c[:, C_ONES:C_ONES + 128] = 1.0
    c[0:64, C_I4:C_I4 + 256] = np.tile(np.eye(64, dtype=np.float32), (1, 4))
    return c


(V_LB0, V_LB1, V_HNORM, V_S5D, V_MU, V_W0, V_A0, V_KK, V_KA, V_RK, V_LNW, V_LNB,
 V_CWX, V_CBX, V_CWB, V_CBB, V_CWC, V_CBC, V_M2D, V_S5) = (0, 1, 2, 3, 4, 8, 9, 10, 11, 12, 13, 14,
                                                            15, 19, 20, 24, 25, 29, 30, 31)
V_SEL = 43
NV = 31 + 12 + 1


def prep_phase1(inp, l, c):
    cs = slice(c * 128, (c + 1) * 128)
    w_in = inp["w_in"][l]
    off = {}
    o = 0
    names = ["q_a", "f_a", "i_a", "z_a", "u_b", "z_b", "rkvwa", "z_c", "xbc", "dt", "z_d", "gate"]
    sizes = [W, W, W, W, W, W, 3 * W + 128, W, W + 1024, 16, W, 4 * D_MODEL]
    for n, s in zip(names, sizes):
        off[n] = o
        o += s
    g = c // 2
    cols = []
    for base in (off["q_a"], off["f_a"], off["i_a"], off["z_a"], off["u_b"],
                 off["rkvwa"], off["rkvwa"] + W, off["rkvwa"] + 2 * W):
        cols.append(np.arange(base + c * 128, base + (c + 1) * 128))
    cols.append(np.arange(off["rkvwa"] + 3 * W, off["rkvwa"] + 3 * W + 128))
    cols.append(np.arange(off["z_c"] + c * 128, off["z_c"] + (c + 1) * 128))
    cols.append(np.arange(off["xbc"] + c * 128, off["xbc"] + (c + 1) * 128))
    cols.append(np.arange(off["xbc"] + W + g * 128, off["xbc"] + W + (g + 1) * 128))
    cols.append(np.arange(off["xbc"] + W + 512 + g * 128, off["xbc"] + W + 512 + (g + 1) * 128))
    cols.append(np.arange(off["z_d"] + c * 128, off["z_d"] + (c + 1) * 128))
    cols.append(np.arange(off["dt"] + 2 * c, off["dt"] + 2 * c + 2))
    cols = np.concatenate(cols)
    w1 = np.ascontiguousarray(w_in[:, cols])
    vec = np.zeros((128, NV), np.float32)
    vec[:, V_LB0] = inp["hgrn_lb_logits"][0][cs]
    vec[:, V_LB1] = inp["hgrn_lb_logits"][1][cs]
    vec[:, V_SEL] = float(l)
    vec[:, V_HNORM] = inp["hgrn_norm"][l][cs]
    vec[:, V_S5D] = inp["s5_d"][l][cs]
    mu = inp["rwkv_mu"][l]
    for k in range(3):
        vec[:, V_MU + k] = mu[k * W + c * 128:k * W + (c + 1) * 128]
    vec[:, V_MU + 3] = mu[3 * W:3 * W + 128]
    for k, n in ((V_W0, "rwkv_w0"), (V_A0, "rwkv_a0"), (V_KK, "rwkv_k_k"), (V_KA, "rwkv_k_a"),
                 (V_RK, "rwkv_r_k"), (V_LNW, "rwkv_ln_w"), (V_LNB, "rwkv_ln_b")):
        vec[:, k] = inp[n][l][cs]
    cw, cb = inp["m2_conv_w"][l], inp["m2_conv_b"][l]
    for k, sl in ((V_CWX, cs), (V_CWB, slice(W + g * 128, W + (g + 1) * 128)),
                  (V_CWC, slice(W + 512 + g * 128, W + 512 + (g + 1) * 128))):
        for j in range(4):
            vec[:, k + j] = cw[j, sl]
        vec[:, k + 4] = cb[sl]
    vec[:, V_M2D] = np.repeat(inp["m2_d"][l][2 * c:2 * c + 2], 64)
    for i in range(4):
        gs = slice(8 * c + 2 * i, 8 * c + 2 * i + 2)
        vec[:, V_S5 + 3 * i] = inp["s5_a_re"][l][gs].reshape(128)
        vec[:, V_S5 + 3 * i + 1] = inp["s5_a_im"][l][gs].reshape(128)
        vec[:, V_S5 + 3 * i + 2] = np.repeat(inp["s5_log_dt"][l][gs], 64)
    rows = np.concatenate([inp["m2_dt_bias"][l][2 * c:2 * c + 2], inp["m2_a_log"][l][2 * c:2 * c + 2]])[None, :]
    lora = np.concatenate([inp["rwkv_w2"][l][:, cs], inp["rwkv_a2"][l][:, cs]], 0)
    s5b = np.zeros((4, 128, 256), np.float32)
    s5c = np.zeros((4, 128, 256), np.float32)
    for i in range(4):
        for gg in range(2):
            G = 8 * c + 2 * i + gg
            gl = 2 * i + gg
            s5b[i, gg * 64:(gg + 1) * 64, gl * 16:(gl + 1) * 16] = inp["s5_b_re"][l][G]
            s5b[i, gg * 64:(gg + 1) * 64, 128 + gl * 16:128 + (gl + 1) * 16] = inp["s5_b_im"][l][G]
            s5c[i, gg * 64:(gg + 1) * 64, gl * 16:(gl + 1) * 16] = inp["s5_c_re"][l][G].T
            s5c[i, gg * 64:(gg + 1) * 64, 128 + gl * 16:128 + (gl + 1) * 16] = inp["s5_c_im"][l][G].T
    gpre = np.ascontiguousarray(inp["norm_pre"][l].reshape(16, 128).T)
    return dict(w1=w1, vecs=vec, rows=np.ascontiguousarray(rows.astype(np.float32)), lora=np.ascontiguousarray(lora),
                s5b=s5b, s5c=s5c, gpre=gpre)


NF, NH = 15, 9


def build_phase1(T, lyr=0, flags=(1, 1, 1, 1)):
    NB = T // TB
    nc = bass.Bass("TRN2", target_bir_lowering=False)
    with ExitStack() as es:
        P = Prog(nc, es)
        x_d = P.dram("xres", [T, D_MODEL], F32, "ExternalInput")
        w_d = P.dram("w1", [D_MODEL, NW1], F32, "ExternalInput")
        gp_d = P.dram("gpre", [128, 16], F32, "ExternalInput")
        vec_d = P.dram("vecs", [128, NV], F32, "ExternalInput")
        row_d = P.dram("rows", [1, 4], F32, "ExternalInput")
        lora_d = P.dram("lora", [128, 128], F32, "ExternalInput")
        s5b_d = P.dram("s5b", [4, 128, 256], F32, "ExternalInput")
        s5c_d = P.dram("s5c", [4, 128, 256], F32, "ExternalInput")
        cst_d = P.dram("consts", [128, NCST], F32, "ExternalInput")
        out_d = P.dram("outs", [4, 128, T], F32, "ExternalOutput")

        cst = P.sb([128, NCST], F32, "cst")
        vec = P.sb([128, NV], F32, "vec")
        gp = P.sb([128, 16], F32, "gp")
        rowb = P.sb([128, 4], F32, "rowb")
        P.dma("sp", cst[:], cst_d[:], writes=[cst])
        P.dma("sp", vec[:], vec_d[:], writes=[vec])
        P.dma("sp", gp[:], gp_d[:], writes=[gp])
        P.dma("sp", rowb[:], row_d.t.partition_broadcast(128), writes=[rowb])
        idb = P.sb([128, 128], BF16, "idb")
        P.cp("dve", idb[:], cst[:, C_ID:C_ID + 128], [cst], [idb])

        def V(k):
            return vec[:, k:k + 1]

        xt = [P.sb([128, D_MODEL], F32, f"xt{i}") for i in range(2)]
        wbf = P.sb([128, 16, NW1], BF16, "wbf")
        w_v = w_d.t.rearrange("(c p) n -> p c n", p=128)
        for kc in range(16):
            st = xt[kc % 2]
            P.dma("sp", st[:, 0:NW1], w_v[:, kc, :], writes=[st])
            P.ts("pool" if kc % 2 else "dve", wbf[:, kc, :], st[:, 0:NW1], gp[:, kc:kc + 1], None, ALU.mult, None, [st, gp], [wbf])

        pa = [P.ps([128, 512], F32, f"pa{i}") for i in range(2)]
        pt = [P.ps([128, 512], F32, f"pt{i}") for i in range(2)]
        pm = [P.ps([128, 512], F32, f"pm{i}") for i in range(2)]
        pqb = [P.ps([128, 512], F32, f"pqb{i}") for i in range(2)]
        pq = [Buf(pqb[i // 4].t[:, (i % 4) * 128:(i % 4 + 1) * 128], f"pq{i}") for i in range(8)]
        for i in range(8):
            pq[i].base = pqb[i // 4]
        pq_alt = [Buf(pm[i // 4].t[:, (i % 4) * 128:(i % 4 + 1) * 128], f"pqa{i}") for i in range(8)]
        for i in range(8):
            pq_alt[i].base = pm[i // 4]

        xn = P.sb([128, D_MODEL], BF16, "xn")
        hT = P.sb([128, 16, TB], BF16, "hT")
        ss = P.sb([128, 1], F32, "ss")
        rstd = P.sb([128, 1], F32, "rstd")
        eps6 = P.sb([128, 1], F32, "eps6")
        P.op("pool", lambda g: g.memset(eps6[:], 1e-6), [], [eps6])
        dtp = P.sb([128, 4, 2], F32, "dtp")
        Fs = [P.sb([128, TB + 4], F32, f"F{i}") for i in range(NF)]
        Hs = [P.sb([128, 2 * TB], BF16, f"H{i}") for i in range(NH)]

        ctx = dict(P=P, cst=cst, vec=vec, V=V, pa=pa, pt=pt, pm=pm, pq=pq, pqb=pqb, F=Fs, H=Hs, idb=idb, rowb=rowb,
                   dtp=dtp, S={}, lyr=lyr, lora_d=lora_d, s5b_d=s5b_d, s5c_d=s5c_d, eps6=eps6, out_d=out_d,
                   hT=hT, wbf=wbf, pq_alt=pq_alt)
        setups = (hgrn_setup, s5_setup, rwkv_setup, ssd_setup)
        blocks = (hgrn_block, s5_block, rwkv_block, ssd_block)
        for b in range(4):
            if flags[b]:
                setups[b](ctx)

        def proj(ci, ncol=128):
            p = pa[ci % 2]
            fns = [lambda g, kc=kc, p=p: g.matmul(p[0:ncol, :], wbf[:, kc, ci * 128:ci * 128 + ncol], hT[:, kc, :],
                                                  start=(kc == 0), stop=(kc == 15)) for kc in range(16)]
            P.group("pe", fns, [wbf, hT], [p])
            return p

        def evac(ci, dst, func=None, eng="act", off=0):
            p = proj(ci)
            if func is not None:
                P.act(dst[:, off:off + TB], p[:], func, [p], [dst])
            else:
                P.cp(eng, dst[:, off:off + TB], p[:], [p], [dst])

        ctx["evac"] = evac

        def stage_a_tile(blk, j):
            t0 = blk * TB
            x = xt[j % 2]
            P.dma("sp", x[:], x_d[t0 + j * 128:t0 + (j + 1) * 128, :], writes=[x])
            P.op("dve", lambda g: g.memset(ss[:], 0.0), [], [ss])
            P.act(xn[:], x[:], AF.Square, [x, ss], [xn, ss], accum=ss[:])
            P.act(rstd[:], ss[:], AF.Sqrt, [ss, eps6], [rstd], bias=eps6[:], scale=1.0 / D_MODEL)
            P.op("dve", lambda g: g.reciprocal(rstd[:], rstd[:]), [rstd], [rstd])
            P.ts("dve", xn[:], x[:], rstd[:], None, ALU.mult, None, [x, rstd], [xn])
            for k4 in range(4):
                p = pt[k4 % 2]
                fns = [lambda g, kc=kc, p=p, k4=k4: g.matmul(p[:, (kc - 4 * k4) * 128:(kc - 4 * k4 + 1) * 128],
                                                             xn[:, kc * 128:(kc + 1) * 128], idb[:], start=True, stop=True)
                       for kc in range(4 * k4, 4 * k4 + 4)]
                P.group("pe", fns, [xn, idb], [p])
                P.cp("act" if k4 % 2 else "dve", hT[:, 4 * k4:4 * k4 + 4, j * 128:(j + 1) * 128],
                     p[:].rearrange("p (a b) -> p a b", b=128), [p], [hT])

        ctx["stage_a_tile"] = stage_a_tile
        ctx["NB"] = NB
        for j in range(4):
            stage_a_tile(0, j)
        for blk in range(NB):
            ctx["t0"] = blk * TB
            if flags[0] and flags[1]:
                hgrn_pre(ctx, blk)
                s5_pre(ctx, blk)
                def _h():
                    for cc in range(8):
                        yield from hgrn_chunk_g(ctx, cc)

                def _s():
                    for i in range(4):
                        yield from s5_tile_g(ctx, i)

                live = [_s(), _h()]
                while live:
                    for g_ in list(live):
                        try:
                            next(g_)
                        except StopIteration:
                            live.remove(g_)
                hgrn_post(ctx)
                s5_post(ctx)
            else:
                for b in (0, 1):
                    if flags[b]:
                        blocks[b](ctx, blk)
            if flags[2] and flags[3]:
                gr, gs = rwkv_gen(ctx, blk), ssd_gen(ctx, blk, True)
                next(gr)
                next(gs)
                live = [gr, gs]
                while live:
                    for g_ in list(live):
                        try:
                            next(g_)
                        except StopIteration:
                            live.remove(g_)
            else:
                if flags[2]:
                    rwkv_block(ctx, blk)
                if flags[3]:
                    ssd_block(ctx, blk)
            if flags[3]:
                pass
            elif blk + 1 < NB:
                for j in range(4):
                    stage_a_tile(blk + 1, j)
        P.finish([out_d])
        print("phase1: ninst", P.ninst, "sbuf bytes/partition", P.sbytes)
    return nc


def store_out(c, b, buf):
    P, t0 = c["P"], c["t0"]
    P.dma("sp", c["out_d"].t[b, :, t0:t0 + TB], buf[:, 0:TB], reads=[buf], writes=[c["out_d"]], key="d_out")


def hgrn_setup(c):
    P, V = c["P"], c["V"]
    S = c["S"]
    lb = P.sb([128, 1], F32, "h_lb")
    oml = P.sb([128, 1], F32, "h_oml")
    noml = P.sb([128, 1], F32, "h_noml")
    P.tt("dve", lb[:], V(V_LB1), V(V_LB0), ALU.subtract, [c["vec"]], [lb])
    P.act(lb[:], lb[:], AF.Sigmoid, [lb], [lb])
    P.tt("dve", lb[:], lb[:], V(V_SEL), ALU.mult, [lb, c["vec"]], [lb])
    P.ts("dve", oml[:], lb[:], -1.0, 1.0, ALU.mult, ALU.add, [lb], [oml])
    P.ts("dve", noml[:], oml[:], -1.0, None, ALU.mult, None, [oml], [noml])
    S["h_lb"], S["h_oml"], S["h_noml"] = lb, oml, noml
    S["h_S32"] = P.sb([128, 128], F32, "h_S32")
    S["h_Sbf"] = P.sb([128, 128], BF16, "h_Sbf")
    P.op("pool", lambda g: g.memset(S["h_S32"][:], 0.0), [], [S["h_S32"]])
    P.op("pool", lambda g: g.memset(S["h_Sbf"][:], 0.0), [], [S["h_Sbf"]])


def hgrn_block(c, blk):
    hgrn_pre(c, blk)
    for cc in range(8):
        hgrn_chunk(c, cc)
    hgrn_post(c)


def _hgrn_names(c):
    P, V, cst, S, pm, pq, F, H = c["P"], c["V"], c["cst"], c["S"], c["pm"], c["pq"], c["F"], c["H"]
    vec, idb, evac = c["vec"], c["idb"], c["evac"]
    lb, oml, noml = S["h_lb"], S["h_oml"], S["h_noml"]
    qs, sig, za, f, b, eb, enb, kt32 = F[0], F[1], F[2], F[3], F[4], F[5], F[6], F[7]
    iv, Qt, Kt, Kh, Vt, KT, PT = H[0], H[1], H[2], H[3], H[4], H[5], H[6]
    o32, sq = F[3], F[4]
    S32, Sbf = S["h_S32"], S["h_Sbf"]
    return locals()


def hgrn_pre(c, blk):
    L = _hgrn_names(c)
    P, V, cst, S, pm, pq, F, H, vec, idb, evac = (L[k] for k in ("P", "V", "cst", "S", "pm", "pq", "F", "H", "vec", "idb", "evac"))
    lb, oml, noml, qs, sig, za, f, b, eb, enb, kt32 = (L[k] for k in ("lb", "oml", "noml", "qs", "sig", "za", "f", "b", "eb", "enb", "kt32"))
    iv, Qt, Kt, Kh, Vt, KT, PT, S32, Sbf = (L[k] for k in ("iv", "Qt", "Kt", "Kh", "Vt", "KT", "PT", "S32", "Sbf"))
    evac(0, qs, AF.Silu)
    evac(1, sig, AF.Sigmoid)
    evac(2, iv, None, "dve")
    evac(3, za, AF.Silu)
    m01 = cst[:, C_M01:C_M01 + TB]
    N = slice(0, TB)
    P.ts("dve", f[:, N], sig[:, N], oml[:], lb[:], ALU.mult, ALU.add, [sig, oml, lb], [f])
    P.act(f[:, N], f[:, N], AF.Ln, [f], [f])
    P.op("dve", lambda g: g.tensor_tensor_scan(b[:, N], m01, f[:, N], 0.0, ALU.mult, ALU.add), [cst, f], [b])
    P.act(eb[:, N], b[:, N], AF.Exp, [b], [eb])
    P.act(enb[:, N], b[:, N], AF.Exp, [b], [enb], scale=-1.0)
    P.ts("dve", kt32[:, N], sig[:, N], noml[:], oml[:], ALU.mult, ALU.add, [sig, noml, oml], [kt32])
    P.tt("dve", kt32[:, N], kt32[:, N], enb[:, N], ALU.mult, [kt32, enb], [kt32])
    P.stt("dve", Qt[:, N], qs[:, N], float(128 ** -0.5), eb[:, N], ALU.mult, ALU.mult, [qs, eb], [Qt])
    P.cp("pool", Kt[:, N], kt32[:, N], [kt32], [Kt])
    k3 = kt32.t[:, 0:TB].rearrange("p (c s) -> p c s", s=64)
    e3 = eb.t[:, 0:TB].rearrange("p (c s) -> p c s", s=64)
    kh3 = Kh.t[:, 0:TB].rearrange("p (c s) -> p c s", s=64)
    P.tt("dve", kh3, k3, e3[:, :, 63:64].broadcast_to([128, 8, 64]), ALU.mult, [kt32, eb], [Kh])
    for (src, dst, pp) in ((iv, Vt, pm[0]), (Kh, KT, pm[1])):
        for half in range(2):
            fns = [lambda g, cc=cc, src=src, pp=pp, half=half: g.matmul(pp[0:64, (cc - 4 * half) * 128:(cc - 4 * half + 1) * 128],
                                                                       src[:, cc * 64:(cc + 1) * 64], idb[:], start=True, stop=True)
                   for cc in range(4 * half, 4 * half + 4)]
            P.group("pe", fns, [src, idb], [pp])
            P.cp("act" if half else "dve", dst[0:64, half * 512:(half + 1) * 512], pp[0:64, :], [pp], [dst])
    sc = pm[0]
    fns = [lambda g, cc=cc: g.matmul(sc[0:64, cc * 64:(cc + 1) * 64], Kt[:, cc * 64:(cc + 1) * 64], Qt[:, cc * 64:(cc + 1) * 64],
                                     start=True, stop=True) for cc in range(8)]
    P.group("pe", fns, [Kt, Qt], [sc])
    P.tt("dve", PT[0:64, N], sc[0:64, :], cst[0:64, C_MH:C_MH + 512], ALU.mult, [sc, cst], [PT])


def hgrn_chunk(c, cc):
    for _ in hgrn_chunk_g(c, cc):
        pass


def hgrn_chunk_g(c, cc):
    L = _hgrn_names(c)
    P, pm, pq = L["P"], L["pm"], L["pq"]
    Qt, Vt, KT, PT, S32, Sbf, eb = (L[k] for k in ("Qt", "Vt", "KT", "PT", "S32", "Sbf", "eb"))
    oT = pm[1]
    cs_ = slice(cc * 64, (cc + 1) * 64)
    vs_ = slice(cc * 128, (cc + 1) * 128)
    P.mm(oT[:, cs_], Vt[0:64, vs_], PT[0:64, cs_], [Vt, PT], [oT], start=True, stop=False)
    P.mm(oT[:, cs_], Sbf[:], Qt[:, cs_], [Sbf, Qt], [oT], start=False, stop=True)
    dS = pq[cc % 2]
    P.mm(dS[:], KT[0:64, vs_], Vt[0:64, vs_], [KT, Vt], [dS])
    yield
    P.stt("dve", S32[:], S32[:], eb[:, cc * 64 + 63:cc * 64 + 64], dS[:], ALU.mult, ALU.add, [S32, eb, dS], [S32])
    P.cp("act", Sbf[:], S32[:], [S32], [Sbf])
    yield


def hgrn_post(c):
    L = _hgrn_names(c)
    P, V, cst, pm, vec = L["P"], L["V"], L["cst"], L["pm"], L["vec"]
    o32, sq, za = L["o32"], L["sq"], L["za"]
    oT = pm[1]
    N = slice(0, TB)
    P.cp("act", o32[:, N], oT[:], [oT], [o32])
    P.tt("pool", sq[:, N], o32[:, N], o32[:, N], ALU.mult, [o32], [sq])
    ssb = pm[0]
    P.mm(ssb[:], cst[:, C_ONES:C_ONES + 128], sq[:, N], [cst, sq], [ssb])
    P.act(sq[:, N], ssb[:], AF.Sqrt, [ssb, c["eps6"]], [sq], bias=c["eps6"][:], scale=1.0 / 128)
    P.op("dve", lambda g: g.reciprocal(sq[:, N], sq[:, N]), [sq], [sq])
    P.stt("dve", o32[:, N], o32[:, N], V(V_HNORM), sq[:, N], ALU.mult, ALU.mult, [o32, vec, sq], [o32])
    P.tt("dve", o32[:, N], o32[:, N], za[:, N], ALU.mult, [o32, za], [o32])
    store_out(c, 0, o32)


def sincos_frac(P, out, tin, ti, R, n):
    P.cp("dve", ti[:, 0:n], tin[:, 0:n], [tin], [ti])
    P.cp("dve", out[:, 0:n], ti[:, 0:n], [ti], [out])
    P.tt("dve", tin[:, 0:n], tin[:, 0:n], out[:, 0:n], ALU.subtract, [tin, out], [tin])
    P.act(out[:, 0:n], tin[:, 0:n], AF.Sin, [tin], [out], scale=6.28318)


def s5_setup(c):
    P, V, cst, S, F, pq = c["P"], c["V"], c["cst"], c["S"], c["F"], c["pq"]
    vec = c["vec"]
    v3 = vec.t[:, V_S5:V_S5 + 12].rearrange("p (i k) -> p i k", k=3)
    a_re, a_im, ldt = v3[:, :, 0], v3[:, :, 1], v3[:, :, 2]
    sm = {n: P.sb([128, 4], F32, "s5_" + n) for n in
          ("lre", "dt", "rho", "th2", "cos", "sin", "c512", "s512", "t1", "t2", "t3", "den", "cre", "cim", "ncim", "stre", "stim", "lr", "li")}
    ti = Buf(F[14].t.bitcast(I32), "s5_ti")
    ti.base = F[14]
    lre, dt, rho, th2 = sm["lre"], sm["dt"], sm["rho"], sm["th2"]
    P.ts("dve", lre[:], a_re, -1e-4, None, ALU.min, None, [vec], [lre])
    P.act(dt[:], ldt, AF.Exp, [vec], [dt])
    P.tt("dve", rho[:], lre[:], dt[:], ALU.mult, [lre, dt], [rho])
    P.act(rho[:], rho[:], AF.Exp, [rho], [rho])
    P.tt("dve", th2[:], a_im, dt[:], ALU.mult, [vec, dt], [th2])
    P.ts("dve", th2[:], th2[:], float(1.0 / (2 * PI)), None, ALU.mult, None, [th2], [th2])
    t1, t2, t3 = sm["t1"], sm["t2"], sm["t3"]
    P.cp("dve", t1[:], th2[:], [th2], [t1])
    sincos_frac(P, sm["sin"], t1, ti, None, 4)
    P.ts("dve", t1[:], th2[:], 0.25, None, ALU.add, None, [th2], [t1])
    sincos_frac(P, sm["cos"], t1, ti, None, 4)
    P.ts("dve", t1[:], th2[:], 512.0, None, ALU.mult, None, [th2], [t1])
    sincos_frac(P, sm["s512"], t1, ti, None, 4)
    P.ts("dve", t1[:], th2[:], 512.0, 0.25, ALU.mult, ALU.add, [th2], [t1])
    sincos_frac(P, sm["c512"], t1, ti, None, 4)
    P.tt("dve", t1[:], rho[:], sm["cos"][:], ALU.mult, [rho, sm["cos"]], [t1])
    P.ts("dve", t1[:], t1[:], -1.0, None, ALU.add, None, [t1], [t1])
    P.tt("dve", t2[:], rho[:], sm["sin"][:], ALU.mult, [rho, sm["sin"]], [t2])
    den = sm["den"]
    P.tt("dve", den[:], lre[:], lre[:], ALU.mult, [lre], [den])
    P.tt("dve", t3[:], a_im, a_im, ALU.mult, [vec], [t3])
    P.tt("dve", den[:], den[:], t3[:], ALU.add, [den, t3], [den])
    P.op("dve", lambda g: g.reciprocal(den[:], den[:]), [den], [den])
    cre, cim, ncim = sm["cre"], sm["cim"], sm["ncim"]
    P.tt("dve", cre[:], t1[:], lre[:], ALU.mult, [t1, lre], [cre])
    P.tt("dve", t3[:], t2[:], a_im, ALU.mult, [t2, vec], [t3])
    P.tt("dve", cre[:], cre[:], t3[:], ALU.add, [cre, t3], [cre])
    P.tt("dve", cre[:], cre[:], den[:], ALU.mult, [cre, den], [cre])
    P.tt("dve", cim[:], t2[:], lre[:], ALU.mult, [t2, lre], [cim])
    P.tt("dve", t3[:], t1[:], a_im, ALU.mult, [t1, vec], [t3])
    P.tt("dve", cim[:], cim[:], t3[:], ALU.subtract, [cim, t3], [cim])
    P.tt("dve", cim[:], cim[:], den[:], ALU.mult, [cim, den], [cim])
    P.ts("dve", ncim[:], cim[:], -1.0, None, ALU.mult, None, [cim], [ncim])
    P.op("pool", lambda g: g.memset(sm["stre"][:], 0.0), [], [sm["stre"]])
    P.op("pool", lambda g: g.memset(sm["stim"][:], 0.0), [], [sm["stim"]])
    S["s5"] = sm
    BT = P.sb([128, 4, 2, 128], BF16, "s5_BT")
    CB = P.sb([128, 4, 2, 128], BF16, "s5_CB")
    S["s5_BT"], S["s5_CB"] = BT, CB
    bst, bb = F[0], F[1]
    ident = cst[:, C_ID:C_ID + 128]
    for i in range(4):
        P.dma("sp", bst[:, 0:256], c["s5b_d"].t[i], writes=[bst])
        P.ts("dve", bb[:, 0:128], bst[:, 0:128], cre[:, i:i + 1], None, ALU.mult, None, [bst, cre], [bb])
        P.stt("dve", bb[:, 0:128], bst[:, 128:256], ncim[:, i:i + 1], bb[:, 0:128], ALU.mult, ALU.add, [bst, ncim, bb], [bb])
        P.ts("dve", bb[:, 128:256], bst[:, 128:256], cre[:, i:i + 1], None, ALU.mult, None, [bst, cre], [bb])
        P.stt("dve", bb[:, 128:256], bst[:, 0:128], cim[:, i:i + 1], bb[:, 128:256], ALU.mult, ALU.add, [bst, cim, bb], [bb])
        for r in range(2):
            P.mm(pq[r][:], bb[:, r * 128:(r + 1) * 128], ident, [bb, cst], [pq[r]])
            P.cp("dve", BT[:, i, r, :], pq[r][:], [pq[r]], [BT])
        P.dma("sp", bst[:, 0:256], c["s5c_d"].t[i], writes=[bst])
        P.cp("dve", CB[:, i, 0, :], bst[:, 0:128], [bst], [CB])
        P.ts("dve", CB[:, i, 1, :], bst[:, 128:256], -1.0, None, ALU.mult, None, [bst], [CB])
    tab = P.sb([128, 4, 2, TB], F32, "s5_tab")
    rht = P.sb([128, 4, TB], F32, "s5_rhot")
    S["s5_tab"], S["s5_rht"] = tab, rht
    iota = cst[:, C_IOTA:C_IOTA + TB]
    tt_, to_ = F[2], F[3]
    for i in range(4):
        P.ts("dve", tt_[:, 0:TB], iota, th2[:, i:i + 1], 0.25, ALU.mult, ALU.add, [cst, th2], [tt_])
        sincos_frac(P, to_, tt_, ti, None, TB)
        P.cp("pool", tab[:, i, 0, :], to_[:, 0:TB], [to_], [tab])
        P.ts("dve", tt_[:, 0:TB], iota, th2[:, i:i + 1], None, ALU.mult, None, [cst, th2], [tt_])
        sincos_frac(P, to_, tt_, ti, None, TB)
        P.cp("pool", tab[:, i, 1, :], to_[:, 0:TB], [to_], [tab])
        P.ts("dve", rht[:, i, :], cst[:, C_ONES:C_ONES + 1].broadcast_to([128, TB]), rho[:, i:i + 1], None, ALU.mult, None, [cst, rho], [rht])


def s5_block(c, blk):
    s5_pre(c, blk)
    for i in range(4):
        s5_tile(c, i)
    s5_post(c)


def _s5_names(c):
    F, H = c["F"], c["H"]
    return dict(u=F[8], cr=F[9], ci=F[10], t3=F[11], t4=F[12], xr=F[13], xi=F[14], y=F[0], ub=H[0], hre=H[2], him=H[3],
                bre=c["pa"][0], bim=c["pa"][1], ybank=c["pt"][0])


def s5_pre(c, blk):
    P, evac = c["P"], c["evac"]
    L = _s5_names(c)
    N = slice(0, TB)
    evac(4, L["u"], None, "dve")
    P.cp("act", L["ub"][:, N], L["u"][:, N], [L["u"]], [L["ub"]])


def s5_tile(c, i):
    for _ in s5_tile_g(c, i):
        pass


def s5_tile_g(c, i):
    P, S = c["P"], c["S"]
    L = _s5_names(c)
    sm, BT, CB, tab, rht = S["s5"], S["s5_BT"], S["s5_CB"], S["s5_tab"], S["s5_rht"]
    N = slice(0, TB)
    ub, cr, ci, t3, t4, xr, xi, hre, him, bre, bim, ybank = (L[k] for k in ("ub", "cr", "ci", "t3", "t4", "xr", "xi", "hre", "him", "bre", "bim", "ybank"))
    cosT, sinT = tab[:, i, 0, :], tab[:, i, 1, :]
    P.mm(bre[:], BT[:, i, 0, :], ub[:, N], [BT, ub], [bre])
    P.mm(bim[:], BT[:, i, 1, :], ub[:, N], [BT, ub], [bim])
    yield
    P.tt("dve", cr[:, N], bre[:], cosT, ALU.mult, [bre, tab], [cr])
    P.tt("dve", t3[:, N], bim[:], sinT, ALU.mult, [bim, tab], [t3])
    P.tt("pool", cr[:, N], cr[:, N], t3[:, N], ALU.add, [cr, t3], [cr])
    P.tt("dve", ci[:, N], bim[:], cosT, ALU.mult, [bim, tab], [ci])
    P.tt("dve", t4[:, N], bre[:], sinT, ALU.mult, [bre, tab], [t4])
    P.tt("pool", ci[:, N], ci[:, N], t4[:, N], ALU.subtract, [ci, t4], [ci])
    yield
    P.op("dve", lambda g, i=i: g.tensor_tensor_scan(xr[:, N], rht[:, i, :], cr[:, N], sm["stre"][:, i:i + 1], ALU.mult, ALU.add),
         [rht, cr, sm["stre"]], [xr])
    P.op("dve", lambda g, i=i: g.tensor_tensor_scan(xi[:, N], rht[:, i, :], ci[:, N], sm["stim"][:, i:i + 1], ALU.mult, ALU.add),
         [rht, ci, sm["stim"]], [xi])
    t1 = sm["t1"]
    P.ts("dve", t1[:, 0:1], xi[:, TB - 1:TB], sm["s512"][:, i:i + 1], None, ALU.mult, None, [xi, sm["s512"]], [t1])
    P.stt("dve", sm["stre"][:, i:i + 1], xr[:, TB - 1:TB], sm["c512"][:, i:i + 1], t1[:, 0:1], ALU.mult, ALU.subtract,
          [xr, sm["c512"], t1], [sm["stre"]])
    P.ts("dve", t1[:, 1:2], xi[:, TB - 1:TB], sm["c512"][:, i:i + 1], None, ALU.mult, None, [xi, sm["c512"]], [t1])
    P.stt("dve", sm["stim"][:, i:i + 1], xr[:, TB - 1:TB], sm["s512"][:, i:i + 1], t1[:, 1:2], ALU.mult, ALU.add,
          [xr, sm["s512"], t1], [sm["stim"]])
    yield
    P.tt("pool", cr[:, N], xr[:, N], cosT, ALU.mult, [xr, tab], [cr])
    P.tt("pool", t3[:, N], xi[:, N], sinT, ALU.mult, [xi, tab], [t3])
    P.tt("dve", hre[:, N], cr[:, N], t3[:, N], ALU.subtract, [cr, t3], [hre])
    P.tt("pool", ci[:, N], xr[:, N], sinT, ALU.mult, [xr, tab], [ci])
    P.tt("pool", t4[:, N], xi[:, N], cosT, ALU.mult, [xi, tab], [t4])
    P.tt("dve", him[:, N], ci[:, N], t4[:, N], ALU.add, [ci, t4], [him])
    yield
    P.mm(ybank[:], CB[:, i, 0, :], hre[:, N], [CB, hre], [ybank], start=(i == 0), stop=False)
    P.mm(ybank[:], CB[:, i, 1, :], him[:, N], [CB, him], [ybank], start=False, stop=(i == 3))
    yield


def s5_post(c):
    P, V, vec = c["P"], c["V"], c["vec"]
    L = _s5_names(c)
    N = slice(0, TB)
    u, y, ybank = L["u"], L["y"], L["ybank"]
    P.stt("dve", y[:, N], u[:, N], V(V_S5D), ybank[:], ALU.mult, ALU.add, [u, vec, ybank], [y])
    P.act(y[:, N], y[:, N], AF.Gelu_apprx_tanh, [y], [y])
    store_out(c, 1, y)


def ssd_setup(c):
    P, S = c["P"], c["S"]
    rowb = c["rowb"]
    S["d_Aneg"] = P.sb([128, 2], F32, "d_Aneg")
    P.act(S["d_Aneg"][:], rowb[:, 2:4], AF.Exp, [rowb], [S["d_Aneg"]])
    P.ts("dve", S["d_Aneg"][:], S["d_Aneg"][:], -1.0, None, ALU.mult, None, [S["d_Aneg"]], [S["d_Aneg"]])
    S["d_S32"] = P.sb([128, 128], F32, "d_S32")
    S["d_Sbf"] = P.sb([128, 128], BF16, "d_Sbf")
    P.op("pool", lambda g: g.memset(S["d_S32"][:], 0.0), [], [S["d_S32"]])
    P.op("pool", lambda g: g.memset(S["d_Sbf"][:], 0.0), [], [S["d_Sbf"]])
    S["d_halo"] = P.sb([128, 3, 3], F32, "d_halo")
    P.op("pool", lambda g: g.memset(S["d_halo"][:], 0.0), [], [S["d_halo"]])
    for n in ("dt", "at", "lcol", "llast", "elast", "wte"):
        S["d_" + n] = P.sb([128, 4, 2], F32, "d_" + n)


def ssd_block(c, blk):
    for _ in ssd_gen(c, blk, False):
        pass


def ssd_gen(c, blk, remap):
    P, V, cst, S, pm, pq, F, H = c["P"], c["V"], c["cst"], c["S"], c["pm"], c["pq"], c["F"], c["H"]
    vec, evac, idb, rowb, dtp = c["vec"], c["evac"], c["idb"], c["rowb"], c["dtp"]
    hT, wbf = c["hT"], c["wbf"]
    N = slice(0, TB)
    if remap:
        xin, Bin, Cin, zd, xc, Ba, Cc, Dm, yo = F[0], F[1], F[2], F[6], F[8], F[9], F[11], F[12], F[13]
        BT, CT, X2, M2 = H[2], H[3], H[4], H[8]
        pq = c["pq_alt"]
    else:
        xin, Bin, Cin, zd, xc, Ba, Cc, Dm, yo = F[0], F[1], F[2], F[3], F[4], F[5], F[6], F[7], F[8]
        BT, CT, X2, M2 = H[0], H[1], H[2], H[3]
    halo = S["d_halo"]
    evac(10, xin, None, "dve", 3)
    evac(11, Bin, None, "act", 3)
    evac(12, Cin, None, "dve", 3)
    evac(13, zd, AF.Silu)
    p = c["pa"][0]
    fns = []
    for j in range(4):
        for kc in range(16):
            fns.append(lambda g, j=j, kc=kc, p=p: g.matmul(p[:, j * 2:j * 2 + 2], hT[:, kc, j * 128:(j + 1) * 128],
                                                           wbf[:, kc, 14 * 128:14 * 128 + 2], start=(kc == 0), stop=(kc == 15)))
    P.group("pe", fns, [wbf, hT], [p])
    P.cp("dve", dtp[:].rearrange("p a b -> p (a b)"), p[:, 0:8], [p], [dtp])
    for k, (src, acc, vw) in enumerate(((xin, xc, V_CWX), (Bin, Ba, V_CWB), (Cin, Cc, V_CWC))):
        P.cp("pool", src[:, 0:3], halo[:, k, :], [halo], [src])
        P.cp("pool", halo[:, k, :], src[:, TB:TB + 3], [src], [halo])
        P.ts("dve", acc[:, N], src[:, 0:TB], V(vw), None, ALU.mult, None, [src, vec], [acc])
        for j in range(1, 4):
            P.stt("dve", acc[:, N], src[:, j:j + TB], V(vw + j), acc[:, N], ALU.mult, ALU.add, [src, vec, acc], [acc])
    P.act(xc[:, N], xc[:, N], AF.Silu, [xc, vec], [xc], bias=V(V_CWX + 4))
    P.act(BT[:, N], Ba[:, N], AF.Silu, [Ba, vec], [BT], bias=V(V_CWB + 4))
    P.act(Cc[:, N], Cc[:, N], AF.Silu, [Cc, vec], [Cc], bias=V(V_CWC + 4))
    P.cp("pool", CT[:, N], Cc[:, N], [Cc], [CT])
    dt, at, lcol, llast, elast, wte = (S["d_" + n] for n in ("dt", "at", "lcol", "llast", "elast", "wte"))
    P.tt("dve", dt[:], dtp[:], rowb[:, 0:2].unsqueeze(1).broadcast_to([128, 4, 2]), ALU.add, [dtp, rowb], [dt])
    P.act(dt[:], dt[:], AF.Exp, [dt], [dt])
    P.act(dt[:], dt[:], AF.Ln, [dt], [dt], bias=1.0)
    P.tt("dve", at[:], dt[:], S["d_Aneg"][:].unsqueeze(1).broadcast_to([128, 4, 2]), ALU.mult, [dt, S["d_Aneg"]], [at])
    tri = cst[:, C_TRI:C_TRI + 128]
    ident = cst[:, C_ID:C_ID + 128]
    neg = cst[:, C_NEG:C_NEG + 128]
    S32, Sbf = S["d_S32"], S["d_Sbf"]
    yield
    for j in range(4):
        ts_ = slice(j * 128, (j + 1) * 128)
        P.mm(pq[0][:, 0:2], tri, at[:, j, :], [cst, at], [pq[0]])
        for h in range(2):
            P.mm(pq[1 + h][:], at[:, j, h:h + 1].broadcast_to([128, 128]), tri, [at, cst], [pq[1 + h]])
        P.mm(pq[3][:], BT[:, ts_], CT[:, ts_], [BT, CT], [pq[3]])
        P.mm(pq[4][:], xc[:, ts_], ident, [xc, cst], [pq[4]])
        P.mm(pq[5][:], BT[:, ts_], idb[:], [BT, idb], [pq[5]])
        yield
        P.cp("act", X2[:, 256:384], pq[5][:], [pq[5]], [X2])
        P.cp("dve", lcol[:, j, :], pq[0][:, 0:2], [pq[0]], [lcol])
        for h in range(2):
            P.cp("dve", llast[:, j, h:h + 1], pq[1 + h][:, 127:128], [pq[1 + h]], [llast])
        P.act(elast[:, j, :], llast[:, j, :], AF.Exp, [llast], [elast])
        for h in range(2):
            P.act(wte[:, j, h:h + 1], lcol[:, j, h:h + 1], AF.Exp, [lcol, llast], [wte], bias=llast[:, j, h:h + 1], scale=-1.0)
        P.tt("dve", wte[:, j, :], wte[:, j, :], dt[:, j, :], ALU.mult, [wte, dt], [wte])
        yield
        for h in range(2):
            hs = slice(h * 64, (h + 1) * 64)
            P.ts("dve", X2[:, h * 64:(h + 1) * 64], pq[4][:, hs], dt[:, j, h:h + 1], None, ALU.mult, None, [pq[4], dt], [X2])
            P.ts("dve", X2[:, 128 + h * 64:128 + (h + 1) * 64], pq[4][:, hs], wte[:, j, h:h + 1], None, ALU.mult, None, [pq[4], wte], [X2])
            d1 = Dm[:, h * 128:(h + 1) * 128]
            P.stt("dve", d1, pq[1 + h][:], lcol[:, j, h:h + 1], neg, ALU.subtract, ALU.add, [pq[1 + h], lcol, cst], [Dm])
            P.act(d1, d1, AF.Exp, [Dm], [Dm])
            P.tt("dve", M2[:, h * 128:(h + 1) * 128], pq[3][:], d1, ALU.mult, [pq[3], Dm], [M2])
            e1 = Dm[:, 256 + h * 128:256 + (h + 1) * 128]
            P.act(e1, pq[1 + h][:], AF.Exp, [pq[1 + h]], [Dm])
            P.tt("pool", M2[:, 256 + h * 128:256 + (h + 1) * 128], Cc[:, ts_], e1, ALU.mult, [Cc, Dm], [M2])
        yield
        for h in range(2):
            hs = slice(h * 64, (h + 1) * 64)
            P.mm(pq[6][hs, :], X2[:, h * 64:(h + 1) * 64], M2[:, h * 128:(h + 1) * 128], [X2, M2], [pq[6]], start=True, stop=False)
            P.mm(pq[6][hs, :], Sbf[:, hs], M2[:, 256 + h * 128:256 + (h + 1) * 128], [Sbf, M2], [pq[6]], start=False, stop=True)
        P.mm(pq[7][:], X2[:, 256:384], X2[:, 128:256], [X2], [pq[7]])
        yield
        for h in range(2):
            hs = slice(h * 64, (h + 1) * 64)
            P.stt("dve", S32[:, hs], S32[:, hs], elast[:, j, h:h + 1], pq[7][:, hs], ALU.mult, ALU.add, [S32, elast, pq[7]], [S32])
        P.cp("act", Sbf[:], S32[:], [S32], [Sbf])
        P.stt("dve", yo[:, ts_], xc[:, ts_], V(V_M2D), pq[6][:], ALU.mult, ALU.add, [xc, vec, pq[6]], [yo])
        if blk + 1 < c["NB"]:
            c["stage_a_tile"](blk + 1, j)
        yield
    P.tt("dve", yo[:, N], yo[:, N], zd[:, N], ALU.mult, [yo, zd], [yo])
    store_out(c, 3, yo)


def rwkv_setup(c):
    P, S, V, vec = c["P"], c["S"], c["V"], c["vec"]
    lw = c["F"][0]
    P.dma("sp", lw[:, 0:128], c["lora_d"][:], writes=[lw])
    S["c_lwb"] = P.sb([128, 2, 128], BF16, "c_lwb")
    P.op("pool", lambda g: g.memset(S["c_lwb"][:], 0.0), [], [S["c_lwb"]])
    P.cp("dve", S["c_lwb"][0:64, 0, :], lw[0:64, 0:128], [lw], [S["c_lwb"]])
    P.cp("dve", S["c_lwb"][64:128, 1, :], lw[64:128, 0:128], [lw], [S["c_lwb"]])
    S["c_omm"] = P.sb([128, 4], F32, "c_omm")
    P.ts("dve", S["c_omm"][:], vec[:, V_MU:V_MU + 4], -1.0, 1.0, ALU.mult, ALU.add, [vec], [S["c_omm"]])
    S["c_H32"] = P.sb([128, 128], F32, "c_H32")
    S["c_Hbf"] = P.sb([128, 128], BF16, "c_Hbf")
    P.op("pool", lambda g: g.memset(S["c_H32"][:], 0.0), [], [S["c_H32"]])
    P.op("pool", lambda g: g.memset(S["c_Hbf"][:], 0.0), [], [S["c_Hbf"]])
    S["c_halo"] = P.sb([128, 4], F32, "c_halo")
    P.op("pool", lambda g: g.memset(S["c_halo"][:], 0.0), [], [S["c_halo"]])
    S["c_eps"] = P.sb([128, 1], F32, "c_eps")
    P.op("pool", lambda g: g.memset(S["c_eps"][:], 64e-5), [], [S["c_eps"]])
    for n in ("M", "MT"):
        S["c_" + n] = [P.sb([128, 512], F32, f"c_{n}{i}") for i in range(2)]
    S["c_Q"] = [P.sb([128, 256], F32, f"c_Q{i}") for i in range(2)]
    S["c_Zb"] = P.sb([128, 1024], BF16, "c_Zb")
    S["c_AT"] = P.sb([128, 2048], BF16, "c_AT")
    S["c_Aak"] = P.sb([128, 1024], BF16, "c_Aak")
    S["c_WT"] = P.sb([128, 64], BF16, "c_WT")
    S["c_Y1"] = P.sb([128, 128], BF16, "c_Y1")
    S["c_Ub"] = P.sb([128, 2, 128], BF16, "c_Ub")
    S["c_Vb"] = P.sb([128, 2, 128], BF16, "c_Vb")
    for n in ("c_Zb", "c_Y1", "c_Ub", "c_Vb"):
        P.op("pool", lambda g, b=S[n]: g.memset(b[:], 0.0), [], [S[n]])


def rwkv_block(c, blk):
    for _ in rwkv_gen(c, blk):
        pass


def rwkv_gen(c, blk):
    P, V, cst, S, pm, pq, F, H = c["P"], c["V"], c["cst"], c["S"], c["pm"], c["pq"], c["F"], c["H"]
    vec, evac, idb = c["vec"], c["evac"], c["idb"]
    pa, pt, pqb = c["pa"], c["pt"], c["pqb"]
    N = slice(0, TB)
    halo, omm, lwb = S["c_halo"], S["c_omm"], S["c_lwb"]
    raw = [F[0], F[1], F[2], F[3]]
    zc = F[4]
    r, k, v, wa = F[5], F[6], F[7], F[8]
    evac(5, raw[0], None, "dve", 1)
    evac(6, raw[1], None, "act", 1)
    evac(7, raw[2], None, "dve", 1)
    evac(8, raw[3], None, "act", 1)
    evac(9, zc, AF.Silu)
    for m, (src, dst) in enumerate(zip(raw, (r, k, v, wa))):
        P.cp("pool", src[:, 0:1], halo[:, m:m + 1], [halo], [src])
        P.cp("pool", halo[:, m:m + 1], src[:, TB:TB + 1], [src], [halo])
        P.ts("dve", dst[:, N], src[:, 1:TB + 1], omm[:, m:m + 1], None, ALU.mult, None, [src, omm], [dst])
        P.stt("dve", dst[:, N], src[:, 0:TB], V(V_MU + m), dst[:, N], ALU.mult, ALU.add, [src, vec, dst], [dst])
    lin = H[0]
    P.act(lin[0:64, N], wa[0:64, N], AF.Tanh, [wa], [lin])
    P.cp("dve", lin[64:128, N], wa[64:128, N], [wa], [lin])
    P.mm(pm[0][:], lwb[:, 0, :], lin[:, N], [lwb, lin], [pm[0]])
    P.mm(pm[1][:], lwb[:, 1, :], lin[:, N], [lwb, lin], [pm[1]])
    logw, iclr, kkn, km = F[0], F[1], F[2], F[3]
    P.act(logw[:, N], pm[0][:], AF.Sigmoid, [pm[0], vec], [logw], bias=V(V_W0))
    P.ts("dve", logw[:, N], logw[:, N], -0.606531, None, ALU.mult, None, [logw], [logw])
    P.act(iclr[:, N], pm[1][:], AF.Sigmoid, [pm[1], vec], [iclr], bias=V(V_A0))
    t9 = F[9]
    P.ts("dve", kkn[:, N], k[:, N], V(V_KK), None, ALU.mult, None, [k, vec], [kkn])
    P.tt("pool", t9[:, N], kkn[:, N], kkn[:, N], ALU.mult, [kkn], [t9])
    bones = cst[:, C_BONES:C_BONES + 128]
    P.mm(pm[0][:], bones, t9[:, N], [cst, t9], [pm[0]])
    P.act(t9[:, N], pm[0][:], AF.Sqrt, [pm[0]], [t9])
    P.ts("dve", t9[:, N], t9[:, N], 1e-12, None, ALU.max, None, [t9], [t9])
    P.op("dve", lambda g: g.reciprocal(t9[:, N], t9[:, N]), [t9], [t9])
    P.tt("dve", kkn[:, N], kkn[:, N], t9[:, N], ALU.mult, [kkn, t9], [kkn])
    P.ts("dve", km[:, N], iclr[:, N], -1.0, V(V_KA), ALU.add, ALU.mult, [iclr, vec], [km])
    P.stt("dve", km[:, N], km[:, N], 1.0, k[:, N], ALU.add, ALU.mult, [km, k], [km])
    cum, g_, gi, gp, E2, kia = F[9], F[10], F[11], F[12], F[13], F[14]
    m01 = cst[:, C_M01:C_M01 + TB]
    P.op("dve", lambda g: g.tensor_tensor_scan(cum[:, N], m01, logw[:, N], 0.0, ALU.mult, ALU.add), [cst, logw], [cum])
    P.act(g_[:, N], cum[:, N], AF.Exp, [cum], [g_])
    P.act(gi[:, N], cum[:, N], AF.Exp, [cum], [gi], scale=-1.0)
    P.tt("pool", gp[:, N], cum[:, N], logw[:, N], ALU.subtract, [cum, logw], [gp])
    P.act(gp[:, N], gp[:, N], AF.Exp, [gp], [gp])

    def v3(buf, off=0, w=64):
        return buf.t[:, off:off + 8 * w].rearrange("p (c s) -> p c s", s=w)

    P.tt("dve", v3(E2), v3(gi), v3(g_)[:, :, 63:64].broadcast_to([128, 8, 64]), ALU.mult, [gi, g_], [E2])
    RA, BK, BKh, vb = H[1], H[2], H[3], H[4]
    ra3 = v3(RA, 0, 128)
    P.tt("dve", ra3[:, :, 0:64], v3(r), v3(g_), ALU.mult, [r, g_], [RA])
    P.stt("dve", ra3[:, :, 64:128], v3(kkn), -1.0, v3(gp), ALU.mult, ALU.mult, [kkn, gp], [RA])
    P.tt("pool", kia[:, N], kkn[:, N], iclr[:, N], ALU.mult, [kkn, iclr], [kia])
    P.tt("dve", BK[:, 0:TB], kia[:, N], gi[:, N], ALU.mult, [kia, gi], [BK])
    P.tt("dve", BK[:, TB:2 * TB], km[:, N], gi[:, N], ALU.mult, [km, gi], [BK])
    BK1 = H[8]
    P.ts("dve", BK1[:, :], BK[:, :], cst[:, C_BONES + 64:C_BONES + 65], None, ALU.mult, None, [BK, cst], [BK1])
    P.ts("dve", BK[:, :], BK[:, :], cst[:, C_BONES:C_BONES + 1], None, ALU.mult, None, [BK, cst], [BK])
    BKm = (BK, BK1)
    P.tt("pool", BKh[:, 0:TB], kia[:, N], E2[:, N], ALU.mult, [kia, E2], [BKh])
    P.tt("pool", BKh[:, TB:2 * TB], km[:, N], E2[:, N], ALU.mult, [km, E2], [BKh])
    P.cp("act", vb[:, N], v[:, N], [v], [vb])
    Atok, Btok, Ktok, Vtok = H[5], H[6], H[7], H[0]
    srcs = ((lambda cc: ra3[:, cc, 64:128], RA, Atok), (lambda cc: BKh[:, cc * 64:(cc + 1) * 64], BKh, Btok),
            (lambda cc: BKh[:, TB + cc * 64:TB + (cc + 1) * 64], BKh, Ktok), (lambda cc: vb[:, cc * 64:(cc + 1) * 64], vb, Vtok))
    n = 0
    for (fsrc, sbuf, dst) in srcs:
        for half in range(2):
            pp = pm[n % 2]
            n += 1
            fns = [lambda g, cc=cc, fsrc=fsrc, pp=pp, half=half: g.matmul(pp[0:64, (cc - 4 * half) * 128:(cc - 4 * half + 1) * 128],
                                                                         fsrc(cc), idb[:], start=True, stop=True)
                   for cc in range(4 * half, 4 * half + 4)]
            P.group("pe", fns, [sbuf, idb], [pp])
            P.cp("act" if half else "dve", dst[0:64, half * 512:(half + 1) * 512], pp[0:64, :], [pp], [dst])
    import os
    stage = float(os.environ.get("RSTAGE", "9"))
    AT, Aak, Zb = S["c_AT"], S["c_Aak"], S["c_Zb"]
    MI = cst[0:64, C_MH:C_MH + 512]
    MS = cst[0:64, C_MR:C_MR + 256]
    ML = cst[0:64, C_MR + 512:C_MR + 768]
    I4 = cst[0:64, C_ID:C_ID + 64].unsqueeze(1).broadcast_to([64, 4, 64])
    for pair in range(2):
        bts = (2 * pair, 2 * pair + 1)
        for bi, bt in enumerate(bts):
            Mk, MkT, Q = S["c_M"][bi], S["c_MT"][bi], S["c_Q"][bi]
            fns = []
            for q4 in range(4):
                cc, h = 2 * bt + q4 // 2, q4 % 2
                ph = slice(h * 64, (h + 1) * 64)
                cs_ = slice(cc * 64, (cc + 1) * 64)
                bt_, kt_ = BKm[h][:, cs_], BKm[h][:, TB + cc * 64:TB + (cc + 1) * 64]
                rt_, at_ = ra3[:, cc, 0:64], ra3[:, cc, 64:128]
                o = q4 * 64
                fns.append(lambda g, a=bt_, b=rt_, o=o: g.matmul(pm[0][0:64, o:o + 64], a, b, start=True, stop=True))
                fns.append(lambda g, a=kt_, b=rt_, o=o: g.matmul(pm[0][0:64, 256 + o:256 + o + 64], a, b, start=True, stop=True))
                fns.append(lambda g, a=bt_, b=at_, o=o: g.matmul(pm[1][0:64, o:o + 64], a, b, start=True, stop=True))
                fns.append(lambda g, a=kt_, b=at_, o=o: g.matmul(pm[1][0:64, 256 + o:256 + o + 64], a, b, start=True, stop=True))
                fns.append(lambda g, a=at_, b=bt_, o=o, bi=bi: g.matmul(pa[bi][0:64, o:o + 64], a, b, start=True, stop=True))
            P.group("pe", fns, [BK, BK1, RA], [pm[0], pm[1], pa[bi]])
            rsub = int(os.environ.get("RSUB", "9"))
            if rsub >= 2:
                P.tt("dve", AT[0:64, bt * 512:(bt + 1) * 512], pm[0][0:64, :], MI, ALU.mult, [pm[0], cst], [AT])
            if rsub >= 3:
                P.tt("dve", Mk[0:64, 0:256], pm[1][0:64, 0:256], MS, ALU.mult, [pm[1], cst], [Mk])
            if rsub >= 4:
                P.tt("dve", Aak[0:64, bt * 256:(bt + 1) * 256], pm[1][0:64, 256:512], MS, ALU.mult, [pm[1], cst], [Aak])
            if rsub >= 5:
                P.tt("dve", MkT[0:64, 0:256], pa[bi][0:64, 0:256], ML, ALU.mult, [pa[bi], cst], [MkT])
            if rsub >= 6:
                P.tt("dve", Q[0:64, 0:256], Mk[0:64, 0:256], cst[0:64, C_I4:C_I4 + 256], ALU.add, [Mk, cst], [Q])
        for rnd in range(6 if stage >= 1 else (1 if stage >= 0.7 else 0)):
            for bi, bt in enumerate(bts):
                Mk, MkT, Q = S["c_M"][bi], S["c_MT"][bi], S["c_Q"][bi]
                cur, nxt = (rnd % 2) * 256, ((rnd + 1) % 2) * 256
                fns = []
                for q4 in range(4):
                    o = q4 * 64
                    m_, mt_ = Mk[0:64, cur + o:cur + o + 64], MkT[0:64, cur + o:cur + o + 64]
                    if rnd < 4:
                        fns.append(lambda g, a=mt_, b=m_, o=o, bi=bi: g.matmul(pa[bi][0:64, o:o + 64], a, b, start=True, stop=True))
                    if rnd < 5:
                        fns.append(lambda g, a=m_, b=mt_, o=o, bi=bi: g.matmul(pa[bi][0:64, 256 + o:256 + o + 64], a, b, start=True, stop=True))
                    if rnd > 0:
                        fns.append(lambda g, a=mt_, o=o, bi=bi, Q=Q: g.matmul(pt[bi][0:64, o:o + 64], a, Q[0:64, o:o + 64], start=True, stop=True))
                P.group("pe", fns, [Mk, MkT, Q], [pa[bi], pt[bi]])
            for bi, bt in enumerate(bts):
                Mk, MkT, Q = S["c_M"][bi], S["c_MT"][bi], S["c_Q"][bi]
                cur, nxt = (rnd % 2) * 256, ((rnd + 1) % 2) * 256
                if rnd > 0:
                    P.tt("dve", Q[0:64, 0:256], Q[0:64, 0:256], pt[bi][0:64, 0:256], ALU.add, [Q, pt[bi]], [Q])
                if rnd < 4:
                    P.cp("act", Mk[0:64, nxt:nxt + 256], pa[bi][0:64, 0:256], [pa[bi]], [Mk])
                if rnd < 5:
                    P.cp("act", MkT[0:64, nxt:nxt + 256], pa[bi][0:64, 256:512], [pa[bi]], [MkT])
        for bi, bt in enumerate(bts):
            P.cp("dve", Zb[0:64, bt * 256:(bt + 1) * 256], S["c_Q"][bi][0:64, 0:256], [S["c_Q"][bi]], [Zb])
    if stage < 2:
        return
    H32, Hbf, WT, Y1, Ub, Vb = S["c_H32"], S["c_Hbf"], S["c_WT"], S["c_Y1"], S["c_Ub"], S["c_Vb"]
    oT = pqb[1]
    OB = pq[4:8]
    yield
    for cc in range(8):
        cs_ = slice(cc * 64, (cc + 1) * 64)
        ck = slice(cc * 128, (cc + 1) * 128)
        P.mm(pq[0][:, :], Atok[0:64, ck], Zb[0:64, cc * 128:(cc + 1) * 128], [Atok, Zb], [pq[0]])
        for h in range(2):
            ph = slice(h * 64, (h + 1) * 64)
            q = cc * 2 + h
            P.mm(pq[1][0:64, ph], Aak[0:64, q * 64:(q + 1) * 64], Vtok[0:64, cc * 128 + h * 64:cc * 128 + (h + 1) * 64], [Aak, Vtok], [pq[1]])
            P.cp("act" if h else "dve", WT[ph, :], pq[0][ph, ph], [pq[0]], [WT])
            P.cp("pool", Vb[0:64, h, ph], Vtok[0:64, cc * 128 + h * 64:cc * 128 + (h + 1) * 64], [Vtok], [Vb])
        P.cp("dve", Y1[0:64, :], pq[1][0:64, :], [pq[1]], [Y1])
        yield
        for h in range(2):
            ph = slice(h * 64, (h + 1) * 64)
            q = cc * 2 + h
            P.mm(pq[2][0:64, ph], Zb[:, q * 64:(q + 1) * 64], Y1[:, ph], [Zb, Y1], [pq[2]], start=True, stop=False)
            P.mm(pq[2][0:64, ph], WT[:, :], Hbf[:, ph], [WT, Hbf], [pq[2]], start=False, stop=True)
        for h in range(2):
            ph = slice(h * 64, (h + 1) * 64)
            P.cp("act" if h else "dve", Ub[0:64, h, ph], pq[2][0:64, ph], [pq[2]], [Ub])
        yield
        P.mm(oT[:, cs_], Hbf[:, :], ra3[:, cc, 0:64], [Hbf, RA], OB, start=True, stop=False)
        for h in range(2):
            q = cc * 2 + h
            ao = (cc // 2) * 512 + (q % 4) * 64
            P.mm(oT[:, cs_], Ub[0:64, h, :], AT[0:64, ao:ao + 64], [Ub, AT], OB, start=False, stop=False)
            P.mm(oT[:, cs_], Vb[0:64, h, :], AT[0:64, 256 + ao:256 + ao + 64], [Vb, AT], OB, start=False, stop=(h == 1))
        for h in range(2):
            P.mm(pq[3][:, :], Btok[0:64, ck], Ub[0:64, h, :], [Btok, Ub], [pq[3]], start=(h == 0), stop=False)
        for h in range(2):
            P.mm(pq[3][:, :], Ktok[0:64, ck], Vb[0:64, h, :], [Ktok, Vb], [pq[3]], start=False, stop=(h == 1))
        yield
        for h in range(2):
            ph = slice(h * 64, (h + 1) * 64)
            P.stt("dve", H32[ph, ph], H32[ph, ph], g_[ph, cc * 64 + 63:cc * 64 + 64], pq[3][ph, ph], ALU.mult, ALU.add, [H32, g_, pq[3]], [H32])
        P.cp("act", Hbf[:], H32[:], [H32], [Hbf])
        yield
    y, yc, sq, rk = F[9], F[10], F[11], F[12]
    P.cp("act", y[:, N], oT[:], OB, [y])
    P.mm(pm[0][:], bones, y[:, N], [cst, y], [pm[0]])
    P.stt("dve", yc[:, N], pm[0][:], -1.0 / 64, y[:, N], ALU.mult, ALU.add, [pm[0], y], [yc])
    P.tt("pool", sq[:, N], yc[:, N], yc[:, N], ALU.mult, [yc], [sq])
    P.mm(pm[1][:], bones, sq[:, N], [cst, sq], [pm[1]])
    P.act(sq[:, N], pm[1][:], AF.Sqrt, [pm[1], S["c_eps"]], [sq], bias=S["c_eps"][:], scale=1.0 / 64)
    P.op("dve", lambda g: g.reciprocal(sq[:, N], sq[:, N]), [sq], [sq])
    P.tt("dve", yc[:, N], yc[:, N], sq[:, N], ALU.mult, [yc, sq], [yc])
    P.ts("dve", yc[:, N], yc[:, N], V(V_LNW), V(V_LNB), ALU.mult, ALU.add, [yc, vec], [yc])
    P.stt("dve", rk[:, N], r[:, N], V(V_RK), km[:, N], ALU.mult, ALU.mult, [r, vec, km], [rk])
    P.mm(pm[0][:], bones, rk[:, N], [cst, rk], [pm[0]])
    P.tt("dve", rk[:, N], pm[0][:], v[:, N], ALU.mult, [pm[0], v], [rk])
    P.tt("dve", yc[:, N], yc[:, N], rk[:, N], ALU.add, [yc, rk], [yc])
    P.tt("dve", yc[:, N], yc[:, N], zc[:, N], ALU.mult, [yc, zc], [yc])
    store_out(c, 2, yc)


NV2 = 80


def prep_phase2(inp, l):
    w_in = inp["w_in"][l]
    off_zb = 5 * W
    off_gate = w_in.shape[1] - 4 * D_MODEL
    wzb = np.ascontiguousarray(w_in[:, off_zb:off_zb + W].reshape(16, 128, 2, 512).transpose(2, 1, 0, 3)).reshape(2, 128, 8192)
    wgate = np.ascontiguousarray(w_in[:, off_gate:].reshape(16, 128, 4, 16, 128).transpose(3, 1, 0, 2, 4)).reshape(16, 128, 8192)
    wup = np.ascontiguousarray(inp["w_up"][l].reshape(4, 8, 128, 16, 128).transpose(3, 2, 0, 1, 4)).reshape(16, 128, 4096)
    wglu = np.ascontiguousarray(inp["s5_w_glu"][l].reshape(8, 128, 1024).transpose(1, 0, 2)).reshape(128, 8192)
    wout = np.ascontiguousarray(inp["w_out"][l].reshape(16, 128, 4, 512).transpose(2, 1, 0, 3)).reshape(4, 128, 8192)
    vec = np.zeros((128, NV2), np.float32)
    vec[:, 0:64] = inp["gate_bias"][l].reshape(4, 16, 128).transpose(2, 0, 1).reshape(128, 64)
    vec[:, 64:72] = inp["s5_b_glu"][l].reshape(8, 128).T
    vec[:, 72:80] = inp["m2_norm"][l].reshape(8, 128).T
    rows = np.ascontiguousarray(np.stack([inp["norm_pre"][l], inp["norm_post"][l]], 0))
    return dict(wzb=wzb, wgate=wgate, wup=wup, wglu=wglu, wout=wout, vec2=vec, rows2=rows)


def build_phase2(Tc):
    NB = Tc // TB
    nc = bass.Bass("TRN2", target_bir_lowering=False)
    with ExitStack() as es:
        P = Prog(nc, es)
        x_d = P.dram("xres", [Tc, D_MODEL], F32, "ExternalInput")
        bo_d = P.dram("bo", [4, W, Tc], F32, "ExternalInput")
        wzb_d = P.dram("wzb", [2, 128, 8192], F32, "ExternalInput")
        wgate_d = P.dram("wgate", [16, 128, 8192], F32, "ExternalInput")
        wup_d = P.dram("wup", [16, 128, 4096], F32, "ExternalInput")
        wglu_d = P.dram("wglu", [128, 8192], F32, "ExternalInput")
        wout_d = P.dram("wout", [4, 128, 8192], F32, "ExternalInput")
        vec_d = P.dram("vec2", [128, NV2], F32, "ExternalInput")
        row_d = P.dram("rows2", [2, D_MODEL], F32, "ExternalInput")
        cst_d = P.dram("consts", [128, NCST], F32, "ExternalInput")
        out_d = P.dram("res_out", [Tc, D_MODEL], F32, "ExternalOutput")

        vec = P.sb([128, NV2], F32, "vec")
        gb = P.sb([128, D_MODEL], F32, "gb")
        npb = P.sb([128, D_MODEL], F32, "npb")
        idf = P.sb([128, 128], F32, "idf")
        idb = P.sb([128, 128], BF16, "idb")
        onesb = P.sb([128, 128], BF16, "onesb")
        P.dma("sp", vec[:], vec_d[:], writes=[vec])
        P.dma("sp", gb[:], row_d.t[0:1, :].partition_broadcast(128), writes=[gb])
        P.dma("sp", npb[:], row_d.t[1:2, :].partition_broadcast(128), writes=[npb])
        P.dma("sp", idf[:], cst_d[:, C_ID:C_ID + 128], writes=[idf])
        P.cp("dve", idb[:], idf[:], [idf], [idb])
        P.op("pool", lambda g: g.memset(onesb[:], 1.0), [], [onesb])
        eps6 = P.sb([128, 1], F32, "eps6")
        eps5 = P.sb([128, 1], F32, "eps5")
        P.op("pool", lambda g: g.memset(eps6[:], 1e-6), [], [eps6])
        P.op("pool", lambda g: g.memset(eps5[:], 1e-5), [], [eps5])

        xt = [P.sb([128, D_MODEL], F32, f"xt{i}") for i in range(2)]
        xn = P.sb([128, D_MODEL], BF16, "xn")
        hT = P.sb([128, 16, TB], BF16, "hT")
        ss = P.sb([128, 1], F32, "ss")
        rstd = P.sb([128, 1], F32, "rstd")
        BO = [P.sb([128, 8, TB], BF16, f"BO{b}") for b in range(4)]
        zbs = P.sb([128, 8, TB], BF16, "zbs")
        wch = [P.sb([128, 8192], BF16, f"wch{i}") for i in range(2)]
        wupb = [P.sb([128, 4096], BF16, f"wupb{i}") for i in range(2)]
        mT = P.sb([128, 16, TB], BF16, "mT")
        Y = [P.sb([128, D_MODEL], F32, f"Y{i}") for i in range(4)]
        G = [P.sb([128, TB], F32, f"G{i}") for i in range(2)]
        acc = P.sb([128, TB], F32, "acc")
        tmp = P.sb([128, TB], F32, "tmp")
        sqb = P.sb([128, TB], BF16, "sqb")

        pt = [P.ps([128, 512], F32, f"pt{i}") for i in range(2)]
        pg = [P.ps([128, 512], F32, f"pg{i}") for i in range(2)]
        pu = [P.ps([128, 512], F32, f"pu{i}") for i in range(2)]
        pz = P.ps([128, 512], F32, "pz")

        nw = [0]

        def load_w(src_ap, n=8192):
            b = wch[nw[0] % 2]
            nw[0] += 1
            P.dma("pool", b[:, 0:n], src_ap, writes=[b])
            return b

        for blk in range(NB):
            t0 = blk * TB
            for j in range(4):
                x = xt[j % 2]
                P.dma("sp", x[:], x_d[t0 + j * 128:t0 + (j + 1) * 128, :], writes=[x])
                P.op("dve", lambda g: g.memset(ss[:], 0.0), [], [ss])
                P.act(xn[:], x[:], AF.Square, [x, ss], [xn, ss], accum=ss[:])
                P.act(rstd[:], ss[:], AF.Sqrt, [ss, eps6], [rstd], bias=eps6[:], scale=1.0 / D_MODEL)
                P.op("dve", lambda g: g.reciprocal(rstd[:], rstd[:]), [rstd], [rstd])
                P.stt("dve", xn[:], x[:], rstd[:], gb[:], ALU.mult, ALU.mult, [x, rstd, gb], [xn])
                for k4 in range(4):
                    p = pt[k4 % 2]
                    fns = [lambda g, kc=kc, p=p, k4=k4: g.matmul(p[:, (kc - 4 * k4) * 128:(kc - 4 * k4 + 1) * 128],
                                                                 xn[:, kc * 128:(kc + 1) * 128], idb[:], start=True, stop=True)
                           for kc in range(4 * k4, 4 * k4 + 4)]
                    P.group("pe", fns, [xn, idb], [p])
                    P.cp("act" if k4 % 2 else "dve", hT[:, 4 * k4:4 * k4 + 4, j * 128:(j + 1) * 128],
                         p[:].rearrange("p (a b) -> p a b", b=128), [p], [hT])
            for b in range(4):
                P.dma("pool", BO[b][:], bo_d.t[b, :, t0:t0 + TB].rearrange("(c p) t -> p c t", p=128), writes=[BO[b]])
            for half in range(2):
                wb = load_w(wzb_d.t[half])
                w3 = wb.t[:, :].rearrange("p (k n) -> p k n", n=512)
                for cj in range(4):
                    c8 = half * 4 + cj
                    p = pg[c8 % 2]
                    fns = [lambda g, kc=kc, p=p, cj=cj, w3=w3: g.matmul(p[:], w3[:, kc, cj * 128:(cj + 1) * 128], hT[:, kc, :],
                                                                        start=(kc == 0), stop=(kc == 15)) for kc in range(16)]
                    P.group("pe", fns, [wb, hT], [p])
                    P.act(G[c8 % 2][:], p[:], AF.Silu, [p], [G[c8 % 2]])
                    P.tt("dve", zbs[:, c8, :], G[c8 % 2][:], BO[1][:, c8, :], ALU.mult, [G[c8 % 2], BO[1]], [zbs])
            wb = load_w(wglu_d[:])
            w3 = wb.t[:, :].rearrange("p (k n) -> p k n", n=1024)
            for c8 in range(8):
                p = pg[c8 % 2]
                fns = [lambda g, kc=kc, p=p, c8=c8, w3=w3: g.matmul(p[:], w3[:, kc, c8 * 128:(c8 + 1) * 128], BO[1][:, kc, :],
                                                                    start=(kc == 0), stop=(kc == 7)) for kc in range(8)]
                P.group("pe", fns, [wb, BO[1]], [p])
                P.act(G[c8 % 2][:], p[:], AF.Sigmoid, [p, vec], [G[c8 % 2]], bias=vec[:, 64 + c8:65 + c8])
                P.tt("dve", zbs[:, c8, :], zbs[:, c8, :], G[c8 % 2][:], ALU.mult, [zbs, G[c8 % 2]], [zbs])
            for grp in range(4):
                for k, cc in enumerate((2 * grp, 2 * grp + 1)):
                    P.tt("pool", sqb[:], BO[3][:, cc, :], BO[3][:, cc, :], ALU.mult, [BO[3]], [sqb])
                    P.mm(pz[:], onesb[:], sqb[:], [onesb, sqb], [pz], start=(k == 0), stop=(k == 1))
                P.act(tmp[:], pz[:], AF.Sqrt, [pz, eps5], [tmp], bias=eps5[:], scale=1.0 / 256)
                P.op("dve", lambda g: g.reciprocal(tmp[:], tmp[:]), [tmp], [tmp])
                for cc in (2 * grp, 2 * grp + 1):
                    P.stt("dve", BO[3][:, cc, :], BO[3][:, cc, :], vec[:, 72 + cc:73 + cc], tmp[:], ALU.mult, ALU.mult,
                          [BO[3], vec, tmp], [BO[3]])
            SRC = (BO[0], zbs, BO[2], BO[3])
            for j in range(16):
                wb = load_w(wgate_d.t[j])
                w3 = wb.t[:, :].rearrange("p (k n) -> p k n", n=512)
                ub = wupb[j % 2]
                P.dma("pool", ub[:], wup_d.t[j], writes=[ub])
                u3 = ub.t[:, :].rearrange("p (k n) -> p k n", n=128)
                for b in range(4):
                    p1, p2 = pg[b % 2], pu[b % 2]
                    fns = [lambda g, kc=kc, p1=p1, b=b, w3=w3: g.matmul(p1[:], w3[:, kc, b * 128:(b + 1) * 128], hT[:, kc, :],
                                                                        start=(kc == 0), stop=(kc == 15)) for kc in range(16)]
                    P.group("pe", fns, [wb, hT], [p1])
                    fns = [lambda g, kc=kc, p2=p2, b=b, u3=u3: g.matmul(p2[:], u3[:, b * 8 + kc, :], SRC[b][:, kc, :],
                                                                        start=(kc == 0), stop=(kc == 7)) for kc in range(8)]
                    P.group("pe", fns, [ub, SRC[b]], [p2])
                    P.act(G[b % 2][:], p1[:], AF.Sigmoid, [p1, vec], [G[b % 2]], bias=vec[:, b * 16 + j:b * 16 + j + 1])
                    if b == 0:
                        P.tt("dve", acc[:], p2[:], G[b % 2][:], ALU.mult, [p2, G[b % 2]], [acc])
                    else:
                        P.tt("dve", tmp[:], p2[:], G[b % 2][:], ALU.mult, [p2, G[b % 2]], [tmp])
                        if b < 3:
                            P.tt("pool", acc[:], acc[:], tmp[:], ALU.add, [acc, tmp], [acc])
                        else:
                            P.tt("dve", mT[:, j, :], acc[:], tmp[:], ALU.add, [acc, tmp], [mT])
            for ec in range(4):
                wb = load_w(wout_d.t[ec])
                w3 = wb.t[:, :].rearrange("p (k n) -> p k n", n=512)
                for tt_ in range(4):
                    p = pu[tt_ % 2]
                    fns = [lambda g, kc=kc, p=p, tt_=tt_, w3=w3: g.matmul(p[:], mT[:, kc, tt_ * 128:(tt_ + 1) * 128], w3[:, kc, :],
                                                                          start=(kc == 0), stop=(kc == 15)) for kc in range(16)]
                    P.group("pe", fns, [wb, mT], [p])
                    P.cp("act" if tt_ % 2 else "dve", Y[tt_][:, ec * 512:(ec + 1) * 512], p[:], [p], [Y[tt_]])
            for tt_ in range(4):
                x = xt[tt_ % 2]
                P.dma("sp", x[:], x_d[t0 + tt_ * 128:t0 + (tt_ + 1) * 128, :], writes=[x])
                P.op("dve", lambda g: g.memset(ss[:], 0.0), [], [ss])
                P.act(xn[:], Y[tt_][:], AF.Square, [Y[tt_], ss], [xn, ss], accum=ss[:])
                P.act(rstd[:], ss[:], AF.Sqrt, [ss, eps6], [rstd], bias=eps6[:], scale=1.0 / D_MODEL)
                P.op("dve", lambda g: g.reciprocal(rstd[:], rstd[:]), [rstd], [rstd])
                P.stt("dve", Y[tt_][:], Y[tt_][:], rstd[:], npb[:], ALU.mult, ALU.mult, [Y[tt_], rstd, npb], [Y[tt_]])
                P.tt("pool", Y[tt_][:], Y[tt_][:], x[:], ALU.add, [Y[tt_], x], [Y[tt_]])
                P.dma("sp", out_d[t0 + tt_ * 128:t0 + (tt_ + 1) * 128, :], Y[tt_][:], reads=[Y[tt_]], writes=[out_d], key="d_out")
        P.finish([out_d])
        print("phase2: ninst", P.ninst, "sbuf bytes/partition", P.sbytes)
    return nc


_PROGS = {}


def kernel(**inp):
    inp = {k: np.asarray(v) for k, v in inp.items()}
    x = inp["x"]
    T = x.shape[1]
    NCORE = 8
    Tc = T // NCORE
    cst = make_consts()
    if ("p1", T) not in _PROGS:
        _PROGS[("p1", T)] = build_phase1(T)
    if ("p2", Tc) not in _PROGS:
        _PROGS[("p2", Tc)] = build_phase2(Tc)
    res = np.ascontiguousarray(x[0], dtype=np.float32)
    for l in range(2):
        in_maps = []
        for c in range(NCORE):
            d = prep_phase1(inp, l, c)
            d["xres"] = res
            d["consts"] = cst
            in_maps.append(d)
        r1 = run_bass_kernel_spmd(_PROGS[("p1", T)], in_maps, core_ids=list(range(NCORE)))
        bo = np.concatenate([r1.results[c]["outs"] for c in range(NCORE)], axis=1)
        d2 = prep_phase2(inp, l)
        in_maps = []
        for c in range(NCORE):
            d = dict(d2)
            d["xres"] = np.ascontiguousarray(res[c * Tc:(c + 1) * Tc])
            d["bo"] = np.ascontiguousarray(bo[:, :, c * Tc:(c + 1) * Tc])
            d["consts"] = cst
            in_maps.append(d)
        r2 = run_bass_kernel_spmd(_PROGS[("p2", Tc)], in_maps, core_ids=list(range(NCORE)))
        res = np.ascontiguousarray(np.concatenate([r2.results[c]["res_out"] for c in range(NCORE)], axis=0))
    return res[None].astype(np.float32)
```
